# Optimizing a Trainium2 kernel written in Bass

```python
import jax, jax.numpy as jnp
from jax import lax
import numpy as np

D_MODEL = 2048
BATCH = 8
SEQ = 4096
DEPTH = 1
DEC_BATCH = 16
DEC_SEQ = 64
PAST_LEN = 2048

CHUNK = 64
D_A = D_MODEL // 2
DK_A = 128
H_A = D_A // DK_A
DV_A = D_A // H_A
D_B = D_MODEL // 2
H_B = 16
DH_B = D_B // H_B
BAND_CHUNKS = 8
BAND_ROWS = BAND_CHUNKS * CHUNK
REL_CLIP = 128
N_REL = 2 * REL_CLIP + 1
EPS = 1e-6
NEG_INF = -1e30
SPLITS = [int(s) for s in np.cumsum([D_A, D_A, D_A, D_A, D_B, D_B, D_B, D_B, D_MODEL])]
D_IN = 4 * D_A + 4 * D_B + 2 * D_MODEL

kernel_name = "hgrn2_chunkband_parallel_streaming_step"


def rms_norm(x, g):
    xf = x.astype(jnp.float32)
    y = xf * lax.rsqrt(jnp.mean(xf * xf, axis=-1, keepdims=True) + EPS)
    return (y * g.astype(jnp.float32)).astype(x.dtype)


def hgrn2_inputs(zq, zf, zi, lb):
    B, T, _ = zq.shape
    q = jax.nn.silu(zq.astype(jnp.float32)).reshape(B, T, H_A, DK_A)
    zf = zf.astype(jnp.float32).reshape(B, T, H_A, DK_A)
    lb = lb.reshape(H_A, DK_A)
    logf = jnp.log(lb + (1.0 - lb) * jax.nn.sigmoid(zf))
    k = (1.0 - lb) * jax.nn.sigmoid(-zf)
    v = zi.astype(jnp.float32).reshape(B, T, H_A, DV_A)
    return q, k, v, logf


def hgrn2_chunked(q, k, v, logf, s0, chunk_len):
    B, T, H, DK = q.shape
    DV = v.shape[-1]
    n = T // chunk_len
    tri = jnp.tril(jnp.ones((chunk_len, chunk_len), dtype=bool))[None, :, :, None, None]

    def to_chunks(a):
        return a.reshape(B, n, chunk_len, H, a.shape[-1]).swapaxes(0, 1)

    def step(S, inp):
        qc, kc, vc, gc = inp
        b = jnp.cumsum(gc, axis=1)
        o_inter = jnp.einsum('blhk,bhkv->blhv', qc * jnp.exp(b), S)
        diff = b[:, :, None] - b[:, None]
        decay = jnp.where(tri, jnp.exp(jnp.where(tri, diff, 0.0)), 0.0)
        scores = jnp.einsum('bthk,bshk,btshk->bhts', qc, kc, decay)
        o_intra = jnp.einsum('bhts,bshv->bthv', scores, vc)
        b_last = b[:, -1]
        S_new = jnp.exp(b_last)[..., None] * S + jnp.einsum(
            'bshk,bshv->bhkv', kc * jnp.exp(b_last[:, None] - b), vc)
        return S_new, o_inter + o_intra

    S_fin, o = lax.scan(step, s0.astype(jnp.float32),
                        (to_chunks(q), to_chunks(k), to_chunks(v), to_chunks(logf)))
    o = o.swapaxes(0, 1).reshape(B, T, H, DV)
    return o, S_fin


def band_attention(q, k, v, q_pos, k_pos, rel_bias):
    s = jnp.einsum('bqhd,bkhd->bhqk', q.astype(jnp.float32), k.astype(jnp.float32)) * (DH_B ** -0.5)
    rel = jnp.clip(q_pos[:, None] - k_pos[None, :], -REL_CLIP, REL_CLIP) + REL_CLIP
    s = s + rel_bias.astype(jnp.float32)[:, rel][None]
    qc = q_pos // CHUNK
    kc = k_pos // CHUNK
    allowed = (k_pos[None, :] >= 0) & (kc[None, :] <= qc[:, None]) & (kc[None, :] >= qc[:, None] - BAND_CHUNKS)
    s = jnp.where(allowed[None, None], s, NEG_INF)
    p = jax.nn.softmax(s, axis=-1)
    return jnp.einsum('bhqk,bkhd->bqhd', p, v.astype(jnp.float32))


def prompt_band_attention(q, k, v, rel_bias):
    B, T, H, DH = q.shape
    n = T // CHUNK
    pad = jnp.zeros((B, BAND_ROWS, H, DH), k.dtype)
    k_pad = jnp.concatenate([pad, k], axis=1)
    v_pad = jnp.concatenate([pad, v], axis=1)
    band = BAND_ROWS + CHUNK

    def one_chunk(c):
        start = c * CHUNK
        qc = lax.dynamic_slice_in_dim(q, start, CHUNK, axis=1)
        kc = lax.dynamic_slice_in_dim(k_pad, start, band, axis=1)
        vc = lax.dynamic_slice_in_dim(v_pad, start, band, axis=1)
        q_pos = start + jnp.arange(CHUNK, dtype=jnp.int32)
        k_pos = start - BAND_ROWS + jnp.arange(band, dtype=jnp.int32)
        return band_attention(qc, kc, vc, q_pos, k_pos, rel_bias)

    o = lax.map(one_chunk, jnp.arange(n, dtype=jnp.int32))
    return o.swapaxes(0, 1).reshape(B, T, H, DH)


def in_proj(x, norm_pre, w_in):
    z = rms_norm(x, norm_pre) @ w_in
    return jnp.split(z, SPLITS, axis=-1)


def merge_out(x, o_a, o_b, ga, gb, ma, mb, gnorm_a, w_proj_a, w_proj_b, w_out, norm_post):
    B, T, _ = x.shape
    o_a = o_a * lax.rsqrt(jnp.mean(o_a * o_a, axis=-1, keepdims=True) + EPS)
    o_a = o_a.reshape(B, T, D_A) * gnorm_a.astype(jnp.float32)
    o_a = (o_a * jax.nn.silu(ga.astype(jnp.float32))).astype(x.dtype)
    o_b = (o_b.reshape(B, T, D_B) * jax.nn.silu(gb.astype(jnp.float32))).astype(x.dtype)
    p_a = o_a @ w_proj_a
    p_b = o_b @ w_proj_b
    merged = (jax.nn.sigmoid(ma.astype(jnp.float32)) * p_a.astype(jnp.float32)
              + jax.nn.sigmoid(mb.astype(jnp.float32)) * p_b.astype(jnp.float32)).astype(x.dtype)
    return x + rms_norm(merged @ w_out, norm_post)


def layer_prompt(x, lb, norm_pre, w_in, gnorm_a, rel_bias, w_proj_a, w_proj_b, w_out, norm_post):
    B, T, _ = x.shape
    qa, fa, ia, ga, qb, kb, vb, gb, ma, mb = in_proj(x, norm_pre, w_in)
    q, k, v, logf = hgrn2_inputs(qa, fa, ia, lb)
    s0 = jnp.zeros((B, H_A, DK_A, DV_A), jnp.float32)
    o_a, s_fin = hgrn2_chunked(q, k, v, logf, s0, CHUNK)
    qh = qb.reshape(B, T, H_B, DH_B)
    kh = kb.reshape(B, T, H_B, DH_B)
    vh = vb.reshape(B, T, H_B, DH_B)
    o_b = prompt_band_attention(qh, kh, vh, rel_bias)
    y = merge_out(x, o_a, o_b, ga, gb, ma, mb, gnorm_a, w_proj_a, w_proj_b, w_out, norm_post)
    rows = min(BAND_ROWS, T)
    return y, s_fin, kh[:, T - rows:], vh[:, T - rows:]


def layer_sample(x, s0, past_k, past_v, lb, norm_pre, w_in, gnorm_a, rel_bias, w_proj_a, w_proj_b, w_out, norm_post):
    B, T, _ = x.shape
    qa, fa, ia, ga, qb, kb, vb, gb, ma, mb = in_proj(x, norm_pre, w_in)
    q, k, v, logf = hgrn2_inputs(qa, fa, ia, lb)
    o_a, s_fin = hgrn2_chunked(q, k, v, logf, s0, T)
    qh = qb.reshape(B, T, H_B, DH_B)
    kh = kb.reshape(B, T, H_B, DH_B)
    vh = vb.reshape(B, T, H_B, DH_B)
    rows = past_k.shape[1]
    k_all = jnp.concatenate([past_k.astype(kh.dtype), kh], axis=1)
    v_all = jnp.concatenate([past_v.astype(vh.dtype), vh], axis=1)
    q_pos = PAST_LEN + jnp.arange(T, dtype=jnp.int32)
    k_pos = jnp.concatenate([PAST_LEN - rows + jnp.arange(rows, dtype=jnp.int32), q_pos])
    o_b = band_attention(qh, k_all, v_all, q_pos, k_pos, rel_bias)
    y = merge_out(x, o_a, o_b, ga, gb, ma, mb, gnorm_a, w_proj_a, w_proj_b, w_out, norm_post)
    return y, s_fin, kh, vh


def setup_inputs(seed: int = 0) -> dict:
    key = jax.random.key(seed)
    ks = jax.random.split(key, 14)
    kv_rows = min(BAND_ROWS, PAST_LEN)
    f32 = jnp.float32
    return {
        "x_prompt": jax.random.normal(ks[0], (BATCH, SEQ, D_MODEL), f32),
        "x_sample": jax.random.normal(ks[1], (DEC_BATCH, DEC_SEQ, D_MODEL), f32),
        "state_hgrn": 0.3 * jax.random.normal(ks[2], (DEPTH, DEC_BATCH, H_A, DK_A, DV_A), f32),
        "cache_k": jax.random.normal(ks[3], (DEPTH, DEC_BATCH, kv_rows, H_B, DH_B), f32),
        "cache_v": jax.random.normal(ks[4], (DEPTH, DEC_BATCH, kv_rows, H_B, DH_B), f32),
        "norm_pre": 1.0 + 0.05 * jax.random.normal(ks[5], (DEPTH, D_MODEL), f32),
        "w_in": jax.random.normal(ks[6], (DEPTH, D_MODEL, D_IN), f32) * D_MODEL ** -0.5,
        "lb_logits": 0.5 * jax.random.normal(ks[7], (DEPTH + 1, D_A), f32),
        "gnorm_a": 1.0 + 0.05 * jax.random.normal(ks[8], (DEPTH, D_A), f32),
        "rel_bias": 0.1 * jax.random.normal(ks[9], (DEPTH, H_B, N_REL), f32),
        "w_proj_a": jax.random.normal(ks[10], (DEPTH, D_A, D_MODEL), f32) * D_A ** -0.5,
        "w_proj_b": jax.random.normal(ks[11], (DEPTH, D_B, D_MODEL), f32) * D_B ** -0.5,
        "w_out": jax.random.normal(ks[12], (DEPTH, D_MODEL, D_MODEL), f32) * D_MODEL ** -0.5,
        "norm_post": 1.0 + 0.05 * jax.random.normal(ks[13], (DEPTH, D_MODEL), f32),
    }


def reference(x_prompt, x_sample, state_hgrn, cache_k, cache_v, norm_pre, w_in, lb_logits,
              gnorm_a, rel_bias, w_proj_a, w_proj_b, w_out, norm_post):
    lb_all = jnp.cumsum(jax.nn.softmax(lb_logits.astype(jnp.float32), axis=0), axis=0)
    xp, xs = x_prompt, x_sample
    sp_list, kp_list, vp_list, ss_list, ks_list, vs_list = [], [], [], [], [], []
    for l in range(DEPTH):
        xp, sp, kp, vp = layer_prompt(xp, lb_all[l], norm_pre[l], w_in[l], gnorm_a[l], rel_bias[l],
                                      w_proj_a[l], w_proj_b[l], w_out[l], norm_post[l])
        xs, ss, ksn, vsn = layer_sample(xs, state_hgrn[l], cache_k[l], cache_v[l], lb_all[l], norm_pre[l],
                                        w_in[l], gnorm_a[l], rel_bias[l], w_proj_a[l], w_proj_b[l],
                                        w_out[l], norm_post[l])
        sp_list.append(sp); kp_list.append(kp); vp_list.append(vp)
        ss_list.append(ss); ks_list.append(ksn); vs_list.append(vsn)
    new_state_prompt = jnp.stack(sp_list)
    new_k_prompt = jnp.stack(kp_list)
    new_v_prompt = jnp.stack(vp_list)
    new_state_sample = jnp.stack(ss_list)
    new_k_sample = jnp.stack(ks_list)
    new_v_sample = jnp.stack(vs_list)
    return (xp, xs, new_state_prompt, new_k_prompt, new_v_prompt, new_state_sample, new_k_sample, new_v_sample)
```

```python
import numpy as np
from contextlib import ExitStack
import concourse.bass as bass
import concourse.mybir as mybir
from concourse.bass_utils import run_bass_kernel_spmd

F32 = mybir.dt.float32
BF16 = mybir.dt.bfloat16
AF = mybir.ActivationFunctionType
ALU = mybir.AluOpType
AX = mybir.AxisListType

D = 2048
KC = 16
DA = 1024
EPS = 1e-6
NCORES = 8
SEQ = 4096
NSAMP = 2
ST = 64

SAME_ENGINE_SYNC = True
RELAX_SAME_ENGINE = False


class T:
    __slots__ = ("name", "w", "r")

    def __init__(self, name):
        self.name = name
        self.w = None
        self.r = {}


class Sched:
    ENGS = ("pe", "act", "dve", "pool", "sp")

    def __init__(self):
        self.prog = {e: [] for e in self.ENGS}
        self.cnt = {e: 0 for e in self.ENGS}
        self.seen = {e: {} for e in self.ENGS}
        self.dma_sems = {}
        self.key_sems = {}

    def new_dma_sem(self):
        k = "d%d" % len(self.dma_sems)
        self.dma_sems[k] = 0
        return k

    def _need(self, eng, k, v, waits):
        if k == eng and (eng == "pe" or not SAME_ENGINE_SYNC):
            return
        if RELAX_SAME_ENGINE and k == eng and eng in ("act", "dve") and v < self.cnt[eng]:
            return
        if self.seen[eng].get(k, 0) >= v:
            return
        if waits.get(k, 0) < v:
            waits[k] = v

    def _deps(self, eng, reads, writes):
        waits = {}
        for t in reads:
            if t.w is not None:
                self._need(eng, t.w[0], t.w[1], waits)
        for t in writes:
            if t.w is not None:
                self._need(eng, t.w[0], t.w[1], waits)
            for k, v in t.r.items():
                self._need(eng, k, v, waits)
        for k, v in waits.items():
            self.seen[eng][k] = v
        return list(waits.items())

    def _record(self, me, reads, writes):
        k, v = me
        for t in reads:
            if t.r.get(k, 0) < v:
                t.r[k] = v
        for t in writes:
            t.w = me
            t.r = {}

    def op(self, eng, fn, reads=(), writes=(), inc=True):
        waits = self._deps(eng, reads, writes)
        if inc:
            self.cnt[eng] += 1
            me = (eng, self.cnt[eng])
        else:
            me = (eng, self.cnt[eng] + 1)
        self.prog[eng].append((waits, fn, (eng, 1) if inc else None))
        self._record(me, reads, writes)
        return me

    def dma(self, q, fn, reads=(), writes=(), sem=None, key=None):
        if sem is None:
            if key is None:
                key = (writes[0] if len(writes) else reads[0]).name
            if key not in self.key_sems:
                self.key_sems[key] = self.new_dma_sem()
            sem = self.key_sems[key]
        waits = self._deps(q, reads, writes)
        self.dma_sems[sem] += 16
        me = (sem, self.dma_sems[sem])
        self.prog[q].append((waits, fn, (sem, 16)))
        self._record(me, reads, writes)
        return me

    def wait_all(self, eng, deps):
        waits = {}
        for k, v in deps:
            if self.seen[eng].get(k, 0) < v and waits.get(k, 0) < v:
                waits[k] = v
        for k, v in waits.items():
            self.seen[eng][k] = v
        if waits:
            self.prog[eng].append((list(waits.items()), None, None))

    def emit(self, nc, stack):
        sems = {}
        for e in self.ENGS:
            sems[e] = stack.enter_context(nc.semaphore("s_" + e))
        for k in self.dma_sems:
            sems[k] = stack.enter_context(nc.semaphore("s_" + k))
        block = stack.enter_context(nc.Block())
        progs = self.prog

        def run(engh, lst):
            for waits, fn, inc in lst:
                for k, v in waits:
                    engh.wait_ge(sems[k], v)
                if fn is not None:
                    ins = fn(engh)
                    if inc is not None:
                        ins.then_inc(sems[inc[0]], inc[1])

        @block.tensor
        def _(e):
            run(e, progs["pe"])

        @block.scalar
        def _(e):
            run(e, progs["act"])

        @block.vector
        def _(e):
            run(e, progs["dve"])

        @block.gpsimd
        def _(e):
            run(e, progs["pool"])

        @block.sync
        def _(e):
            run(e, progs["sp"])


class Arena:
    def __init__(self, ap):
        self.ap = ap
        self.live = []

    def take(self, name, off, nbytes):
        assert off % 4 == 0 and nbytes % 4 == 0
        assert off + nbytes <= self.ap.shape[1] * 4, (name, off, nbytes)
        t = T(name)
        for (s, e, old) in self.live:
            if s < off + nbytes and off < e:
                if old.w is not None:
                    k, v = old.w
                    if t.r.get(k, 0) < v:
                        t.r[k] = v
                for k, v in old.r.items():
                    if t.r.get(k, 0) < v:
                        t.r[k] = v
        self.live = [(s, e, o) for (s, e, o) in self.live if not (s >= off and e <= off + nbytes)]
        self.live.append((off, off + nbytes, t))
        return t, self.ap[:, off // 4:(off + nbytes) // 4]


class WStream:
    def __init__(self, S, slots, plan):
        self.S = S
        self.slots = slots
        self.plan = plan
        self.rec = []
        self.n_acq = 0
        self.n_issued = 0
        self.n_rel = 0
        self.n_req = None
        self.tscr = []

    def _issue(self):
        if self.plan is None or self.n_issued >= len(self.plan):
            return
        idx = self.n_issued
        src = self.plan[idx]
        si = idx % len(self.slots)
        ap, t, sem = self.slots[si]
        nk, ncol = src.shape[1], src.shape[2]
        dst = ap[:, 0:nk, 0:ncol]
        if self.n_req is None:
            self.S.dma("pool", lambda e, dst=dst, src=src: e.dma_start(out=dst, in_=src), writes=[t], sem=sem)
        else:
            j = idx % self.n_req
            scr = self.plan.scratch(idx)
            if idx < self.n_req:
                self.S.dma("pool", lambda e, dst=dst, src=src: e.dma_start(out=dst, in_=src), writes=[t], sem=sem)
                self.S.dma("sp", lambda e, dst=dst, scr=scr: e.dma_start(out=scr, in_=dst), reads=[t], writes=[self.tscr[j]],
                           key="scrw%d" % si)
            else:
                self.S.dma("pool", lambda e, dst=dst, scr=scr: e.dma_start(out=dst, in_=scr), reads=[self.tscr[j]], writes=[t],
                           sem=sem)
        self.n_issued += 1

    def start(self):
        for _ in range(len(self.slots)):
            self._issue()

    def acquire(self, src):
        self.rec.append(src)
        ap, t, sem = self.slots[self.n_acq % len(self.slots)]
        self.n_acq += 1
        return ap, t

    def release(self, n=1):
        for _ in range(n):
            self.n_rel += 1
            self._issue()


def _emit_program(nc, S, st, plan, cfg):
    NPT = cfg["n_prompt_tiles"]
    dbg = cfg.get("debug", False)

    def din(name, shape):
        return nc.dram_tensor(name, shape, F32, kind="ExternalInput").ap()

    def dout(name, shape):
        return nc.dram_tensor(name, shape, F32, kind="ExternalOutput").ap()

    xp = din("xp", [NPT * 128, D])
    xs = din("xs", [NSAMP, ST, D])
    st_in = din("st_in", [NSAMP, 8, 128, 128])
    ck = din("ck", [NSAMP, 512, 1024])
    cv = din("cv", [NSAMP, 512, 1024])
    w_in = din("w_in", [D, 12288])
    w_pa = din("w_pa", [DA, D])
    w_pb = din("w_pb", [DA, D])
    w_out = din("w_out", [D, D])
    norm_pre = din("norm_pre", [1, D])
    norm_post = din("norm_post", [1, D])
    gnorm = din("gnorm", [1, DA])
    lb_logits = din("lb_logits", [2, DA])
    rel_bias = din("rel_bias", [16, 257])

    yp = dout("yp", [NPT * 128, D])
    ys = dout("ys", [NSAMP, ST, D])
    sp_o = dout("sp_o", [8, 128, 128])
    kp = dout("kp", [512, 1024])
    vp = dout("vp", [512, 1024])
    ss_o = dout("ss_o", [NSAMP, 8, 128, 128])
    ks = dout("ks", [NSAMP, ST, 1024])
    vs = dout("vs", [NSAMP, ST, 1024])
    rb_ext = nc.dram_tensor("rb_ext", [16, 384], F32, kind="Internal").ap()
    w_in_bf = nc.dram_tensor("w_in_bf", [D, 12288], BF16, kind="Internal").ap()
    w_pa_bf = nc.dram_tensor("w_pa_bf", [DA, D], BF16, kind="Internal").ap()
    w_pb_bf = nc.dram_tensor("w_pb_bf", [DA, D], BF16, kind="Internal").ap()
    w_out_bf = nc.dram_tensor("w_out_bf", [D, D], BF16, kind="Internal").ap()

    w_in_v = w_in.rearrange("(k p) n -> p k n", p=128)
    w_pa_v = w_pa.rearrange("(k p) n -> p k n", p=128)
    w_pb_v = w_pb.rearrange("(k p) n -> p k n", p=128)
    w_out_v = w_out.rearrange("(k p) n -> p k n", p=128)

    def sb(name, shape, dt=F32):
        return st.enter_context(nc.sbuf_tensor(name, shape, dt))

    NW = cfg.get("nw", 4)
    wslots = []
    for i in range(NW):
        ap = sb("wslot%d" % i, [128, 8, 512], BF16)
        wslots.append((ap, T("wslot%d" % i), S.new_dma_sem()))
    W = WStream(S, wslots, plan)

    ARENA_BYTES = 104 * 1024
    arena_ap = sb("arena", [128, ARENA_BYTES // 4], F32)
    AR = Arena(arena_ap)
    LOC = 32 * 1024
    _t0, _a0 = AR.take("cst16", LOC + 48 * 1024, 1536)
    cst16 = _a0[0:16, :]
    _t1, _a1 = AR.take("rbsb", LOC + 50 * 1024, 1536)
    rbsb = _a1[0:16, :]

    kT = sb("kT", [128, 8, 1024], BF16)
    vaug = sb("vaug", [128, 8, 16, 65], BF16)
    S32 = sb("S32", [128, 8, 128], F32)
    Sbf = sb("Sbf", [128, 8, 128], BF16)
    gpost_b = sb("gpost_b", [128, D], F32)
    gnorm_b = sb("gnorm_b", [128, DA], F32)
    tabE = sb("tabE", [128, 16, 256], BF16)
    rmask = sb("rmask", [128, 512], F32)
    ident_f = sb("ident_f", [128, 128], F32)
    ident_b = sb("ident_b", [128, 128], BF16)
    mask01 = sb("mask01", [128, 64], F32)
    gpreT = sb("gpreT", [128, 16], F32)
    lbT = sb("lbT", [128, 8], F32)
    lnoml = sb("lnoml", [128, 8], F32)
    small = sb("small", [128, 64], F32)
    epsc = sb("epsc", [128, 1], F32)
    onec = sb("onec", [128, 1], F32)
    junkS = sb("junkS", [128, D], BF16)
    T_junk = T("junkS")
    PREF = {}
    EL = sb("EL", [128, 8, 8], F32)
    stats = sb("stats", [128, 64], F32)
    rbc = sb("rbc", [16, 1], F32)
    Jrev = sb("Jrev", [128, 128], F32)

    ps = st.enter_context(nc.psum_tensor("ps", [128, 8, 512], F32))
    PB = [T("bank%d" % b) for b in range(8)]

    T_kT = [T("kT%d" % i) for i in range(8)]
    T_va = [T("va%d" % i) for i in range(8)]
    T_S32 = [T("S32_%d" % h) for h in range(8)]
    T_Sbf = [T("Sbf_%d" % h) for h in range(8)]
    T_c = T("consts")
    T_EL = [T("EL%d" % h) for h in range(8)]
    T_stats = T("stats")
    T_small = T("small")
    T_rbext = T("rbext")


    out_deps = []

    r_ = NPT - 2
    sizes = [4] * (r_ // 4) + [2] * ((r_ % 4) // 2) + [2]
    if cfg.get("sizes"):
        sizes = list(cfg["sizes"])
    passes = []
    g0 = 0
    for pi_, sz in enumerate(sizes):
        tl_ = [dict(kind="p", gt=g0 + i, nt=128, col=128 * i) for i in range(sz)]
        g0 += sz
        if pi_ == len(sizes) - 1:
            tl_ += [dict(kind="s", s=i, nt=ST, col=128 * sz + ST * i) for i in range(NSAMP)]
        passes.append(tl_)
    assert g0 == NPT

    out_tiles = set(range(NPT - 4, NPT))

    def xsrc(tl):
        if tl["kind"] == "p":
            return xp[tl["gt"] * 128:(tl["gt"] + 1) * 128, :]
        return xs[tl["s"], :, :]

    def ydst(tl):
        if tl["kind"] == "p":
            return yp[tl["gt"] * 128:(tl["gt"] + 1) * 128, :]
        return ys[tl["s"], :, :]

    _pass0_tiles = passes[0]
    cfg["_sizes"] = list(sizes)

    def A(eng, fn, reads=(), writes=(), inc=True):
        return S.op(eng, fn, reads=reads, writes=writes, inc=inc)

    A("pool", lambda e: e.memset(ident_f[:], 0.0), writes=[T_c])
    A("pool", lambda e: e.affine_select(out=ident_f[:], in_=ident_f[:], compare_op=ALU.not_equal, fill=1.0,
                                        base=0, pattern=[[-1, 128]], channel_multiplier=1),
      reads=[T_c], writes=[T_c])
    A("pool", lambda e: e.tensor_copy(ident_b[:], ident_f[:]), reads=[T_c], writes=[T_c])
    A("pool", lambda e: e.memset(mask01[:], 1.0), writes=[T_c])
    A("pool", lambda e: e.affine_select(out=mask01[0:64, :], in_=mask01[0:64, :], compare_op=ALU.is_ge, fill=0.0,
                                        base=0, pattern=[[1, 64]], channel_multiplier=-1),
      reads=[T_c], writes=[T_c])
    S.dma("sp", lambda e: e.dma_start(out=mask01[64:128, :], in_=mask01[0:64, :]), reads=[T_c], writes=[T_c], key="c_mask")
    A("pool", lambda e: e.memset(rmask[:], 1.0), writes=[T_c])
    A("pool", lambda e: e.memset(rmask[:].rearrange("p (c t) -> p c t", t=64)[:, :, 0:1], 0.0),
      reads=[T_c], writes=[T_c])
    A("pool", lambda e: e.memset(epsc[:], EPS), writes=[T_c])
    A("pool", lambda e: e.memset(onec[:], 1.0), writes=[T_c])
    A("pool", lambda e: e.memset(vaug[:], 1.0), writes=T_va)
    A("pool", lambda e: e.memset(S32[:], 0.0), writes=T_S32)
    A("pool", lambda e: e.memset(Sbf[:], 0.0), writes=T_Sbf)

    S.dma("sp", lambda e: e.dma_start(out=gpost_b[:], in_=norm_post.partition_broadcast(128)), writes=[T_c], key="c_gpost")
    S.dma("sp", lambda e: e.dma_start(out=gnorm_b[:], in_=gnorm.partition_broadcast(128)), writes=[T_c], key="c_gnorm")
    T_cst16 = _t0
    S.dma("sp", lambda e: e.dma_start(out=cst16[:, 0:128], in_=norm_pre.rearrange("o (k p) -> (o k) p", p=128)),
          writes=[T_cst16])
    S.dma("sp", lambda e: e.dma_start(out=cst16[:, 128:256], in_=lb_logits.rearrange("r (h p) -> (r h) p", p=128)),
          writes=[T_cst16])
    A("pe", lambda e: e.transpose(out=ps[:, 0, 0:16], in_=cst16[:, 0:128], identity=ident_f[0:16, 0:16]),
      reads=[T_c, T_cst16], writes=[PB[0]])
    A("pe", lambda e: e.transpose(out=ps[:, 0, 16:32], in_=cst16[:, 128:256], identity=ident_f[0:16, 0:16]),
      reads=[T_c, T_cst16], writes=[PB[0]])
    A("dve", lambda e: e.tensor_copy(gpreT[:], ps[:, 0, 0:16]), writes=[PB[0], T_c])
    A("dve", lambda e: e.tensor_copy(small[:, 0:16], ps[:, 0, 16:32]), writes=[PB[0], T_small])
    A("dve", lambda e: e.tensor_tensor(out=small[:, 16:24], in0=small[:, 8:16], in1=small[:, 0:8], op=ALU.subtract),
      reads=[T_small], writes=[T_small])
    A("act", lambda e: e.activation(out=small[:, 24:32], in_=small[:, 16:24], func=AF.Exp),
      reads=[T_small], writes=[T_small])
    A("dve", lambda e: e.tensor_scalar_add(small[:, 32:40], small[:, 24:32], 1.0), reads=[T_small], writes=[T_small])
    A("dve", lambda e: e.reciprocal(lbT[:], small[:, 32:40]), reads=[T_small], writes=[T_c])
    A("dve", lambda e: e.tensor_tensor(out=small[:, 40:48], in0=small[:, 24:32], in1=lbT[:], op=ALU.mult),
      reads=[T_small, T_c], writes=[T_small])
    A("act", lambda e: e.activation(out=lnoml[:], in_=small[:, 40:48], func=AF.Ln), reads=[T_small], writes=[T_c])

    A("pool", lambda e: e.memset(Jrev[:], 0.0), writes=[T_c])
    A("pool", lambda e: e.affine_select(out=Jrev[:], in_=Jrev[:], compare_op=ALU.not_equal, fill=1.0,
                                        base=-127, pattern=[[1, 128]], channel_multiplier=1),
      reads=[T_c], writes=[T_c])

    init_deps = [(e_, S.cnt[e_]) for e_ in S.ENGS if S.cnt[e_] > 0] + [(k_, S.dma_sems[k_]) for k_ in S.key_sems.values()
                                                                        if S.dma_sems[k_] > 0]
    for e_ in ("pe", "act", "dve"):
        S.wait_all(e_, init_deps)
    if plan is not None:
        for ap_ in (w_in, w_pa, w_pb, w_out, w_in_bf, w_pa_bf, w_pb_bf, w_out_bf):
            plan.bind(ap_)
        if cfg.get("scratch", True):
            n_passes_ = len(cfg["_sizes"])
            assert len(plan) % n_passes_ == 0
            W.n_req = len(plan) // n_passes_
            W.tscr = [T("scr%d" % j_) for j_ in range(W.n_req)]
    for i0_, tl0_ in enumerate(_pass0_tiles):
        t_x0, x_ap0 = AR.take("x%d" % i0_, LOC + i0_ * 8192, 8192)
        S.dma("sp", lambda e, d=x_ap0[0:tl0_["nt"], :], s=xsrc(tl0_): e.dma_start(out=d, in_=s), writes=[t_x0])
        PREF[i0_] = (t_x0, x_ap0)

    T_rbsb = _t1
    T_tab = T("tabE")
    traw_ap = kT[:].rearrange("p c t -> p (c t)").bitcast(F32)
    traw = traw_ap.rearrange("p (h c) -> p h c", c=256)

    def table_dma():
        S.dma("sp", lambda e: e.dma_start(out=rbsb[:, 0:257], in_=rel_bias), writes=[T_rbsb])
        A("dve", lambda e: e.tensor_copy(rbsb[:, 257:384], rbsb[:, 256:257].to_broadcast([16, 127])), writes=[T_rbsb])
        A("dve", lambda e: e.tensor_copy(rbc[:], rbsb[:, 256:257]), reads=[T_rbsb], writes=[T_small])
        A("dve", lambda e: e.tensor_scalar(out=rbsb[:], in0=rbsb[:], scalar1=rbc[:, 0:1], scalar2=None, op0=ALU.subtract),
          reads=[T_small], writes=[T_rbsb])
        S.dma("sp", lambda e: e.dma_start(out=rb_ext, in_=rbsb[:]), reads=[T_rbsb], writes=[T_rbext])
        for h in range(16):
            for r in range(2):
                base = h * 384 + (129 if r == 0 else 1)
                src = bass.AP(tensor=rb_ext.tensor, offset=rb_ext.offset + base, ap=[[1, 128], [1, 128]])
                dst = traw[:, h, r * 128:(r + 1) * 128]
                S.dma("sp", lambda e, dst=dst, src=src: e.dma_start(out=dst, in_=src), reads=[T_rbext], writes=T_kT,
                      key="tab_raw")

    def table_compute():
        for q in range(8):
            bq = 5 + (q % 2)
            A("pe", lambda e, q=q, bq=bq: e.matmul(ps[:, bq, :], lhsT=Jrev[:], rhs=traw_ap[:, q * 512:(q + 1) * 512], start=True, stop=True),
              reads=[T_c] + T_kT, writes=[PB[bq]])
            A("act", lambda e, q=q, bq=bq: e.activation(out=tabE[:, 2 * q:2 * q + 2, :], in_=ps[:, bq, :].rearrange("p (h c) -> p h c", c=256),
                                                         func=AF.Exp),
              writes=[PB[bq], T_tab])
        A("dve", lambda e: e.memset(tabE[64:128, :, 128:192], 0.0), writes=[T_tab])

    bank_rr = [0]

    def rr_bank(lo, hi):
        b = lo + (bank_rr[0] % (hi - lo))
        bank_rr[0] += 1
        return b

    def do_pass(pi, tiles):
        ntl = len(tiles)
        TT = sum(t["nt"] for t in tiles)
        nchunk = TT // 64

        T_xn, xn_ap = [], None
        t_, xn_raw = AR.take("xnT", 0, 16 * 1024)
        xnT = xn_raw.bitcast(BF16).rearrange("p (k t) -> p k t", t=512)
        T_xn = [t_]
        t_oa, oa_raw = AR.take("oaT", 16 * 1024, 8 * 1024)
        oaT = oa_raw.bitcast(BF16).rearrange("p (k t) -> p k t", t=512)
        t_ob, ob_raw = AR.take("obT", 24 * 1024, 8 * 1024)
        obT = ob_raw.bitcast(BF16).rearrange("p (k t) -> p k t", t=512)

        xt = []
        for i, tl in enumerate(tiles):
            if i in PREF:
                xt.append(PREF.pop(i))
                continue
            t_x, x_ap = AR.take("x%d" % i, LOC + i * 8192, 8192)
            xt.append((t_x, x_ap))
            nt = tl["nt"]
            S.dma("sp", lambda e, d=x_ap[0:nt, :], s=xsrc(tl): e.dma_start(out=d, in_=s), writes=[t_x])
        t_junk, junk = T_junk, junkS
        for i, tl in enumerate(tiles):
            nt, col = tl["nt"], tl["col"]
            t_x, x_ap = xt[i]
            ssq = stats[0:nt, 0:1]
            sd = stats[0:nt, 1:2]
            rs = stats[0:nt, 2:3]
            A("act", lambda e, x=x_ap[0:nt, :], j=junk[0:nt, :], a=ssq: e.activation(out=j, in_=x, func=AF.Square, accum_out=a),
              reads=[t_x], writes=[t_junk, T_stats])
            A("act", lambda e, a=ssq, o=sd, n=nt: e.activation(out=o, in_=a, func=AF.Sqrt, scale=1.0 / D, bias=epsc[0:n, :]),
              reads=[T_c], writes=[T_stats])
            A("dve", lambda e, o=rs, i_=sd: e.reciprocal(o, i_), writes=[T_stats])
            A("dve", lambda e, x=x_ap[0:nt, :], r=rs: e.tensor_scalar(out=x, in0=x, scalar1=r, scalar2=None, op0=ALU.mult),
              reads=[T_stats], writes=[t_x])
            for kb in range(4):
                b = 6 + (kb % 2)
                for kk in range(4):
                    k = kb * 4 + kk
                    A("pe", lambda e, b=b, kk=kk, k=k, x=x_ap, nt=nt: e.transpose(
                        out=ps[:, b, kk * nt:(kk + 1) * nt], in_=x[0:nt, k * 128:(k + 1) * 128], identity=ident_f[0:nt, 0:nt]),
                      reads=[t_x, T_c], writes=[PB[b]], inc=(kk == 3))
                A("dve", lambda e, b=b, kb=kb, nt=nt, col=col: e.tensor_tensor(
                    out=xnT[:, kb * 4:(kb + 1) * 4, col:col + nt],
                    in0=ps[:, b, 0:4 * nt].rearrange("p (a t) -> p a t", t=nt),
                    in1=gpreT[:, kb * 4:(kb + 1) * 4].unsqueeze(2).to_broadcast([128, 4, nt]), op=ALU.mult),
                  reads=[T_c], writes=[PB[b], T_xn[0]])

        off = LOC
        t_qt, qt_raw = AR.take("qt", off, 8192); off += 8192
        qt = qt_raw.bitcast(BF16).rearrange("p (h t) -> p h t", t=512)
        t_kt, kt_raw = AR.take("kt", off, 8192); off += 8192
        kt = kt_raw.bitcast(BF16).rearrange("p (h t) -> p h t", t=512)
        vsb, Gsb = [], []
        for i in range(ntl):
            t1, a1 = AR.take("vsb%d" % i, off, 2048); off += 2048
            vsb.append((t1, a1.bitcast(BF16)))
        for i in range(ntl):
            t1, a1 = AR.take("G%d" % i, off, 2048); off += 2048
            Gsb.append((t1, a1.bitcast(BF16)))
        off = LOC + 32768
        t_B4, B4_raw = AR.take("B4", off, 8192); off += 8192
        B4 = B4_raw.rearrange("p (h t) -> p h t", t=512)
        tmp = {}
        for nm in ("U", "L1", "L2", "Wq", "L3"):
            for par in range(2):
                tmp[nm, par] = AR.take("%s_%d" % (nm, par), off, 2048); off += 2048
        assert off == LOC + 60 * 1024
        t_sqj1, sqj1 = AR.take("sqj1", LOC + 68 * 1024, 2048)
        B4_off = LOC + 32768

        def wsrc_in(c0, kh):
            return w_in_v[:, kh * 8:(kh + 1) * 8, c0:c0 + 512]

        def proj_fm(bank, wl, cl, rhs_all, n):
            nk = 8 * len(wl)
            for k in range(nk):
                wap, wt = wl[k // 8]
                A("pe", lambda e, bank=bank, wap=wap, k=k, cl=cl, n=n, rhs_all=rhs_all, nk=nk: e.matmul(
                    ps[:, bank, 0:n], lhsT=wap[:, k % 8, cl * 128:(cl + 1) * 128], rhs=rhs_all[:, k, 0:n],
                    start=(k == 0), stop=(k == nk - 1)),
                  reads=[wt, rd_T], writes=[PB[bank]], inc=(k == nk - 1))

        def proj_tm(bank, wl, lhs_all, col, nt, ncol=512):
            nk = 8 * len(wl)
            for k in range(nk):
                wap, wt = wl[k // 8]
                A("pe", lambda e, bank=bank, wap=wap, k=k, col=col, nt=nt, lhs_all=lhs_all, nk=nk, ncol=ncol: e.matmul(
                    ps[0:nt, bank, 0:ncol], lhsT=lhs_all[:, k, col:col + nt], rhs=wap[:, k % 8, 0:ncol],
                    start=(k == 0), stop=(k == nk - 1)),
                  reads=[wt, rd_T], writes=[PB[bank]], inc=(k == nk - 1))

        rd_T = T_xn[0]
        for hg in range(2):
            wl = [W.acquire(wsrc_in(1024 + hg * 512, 0)), W.acquire(wsrc_in(1024 + hg * 512, 1))]
            fbank = {}

            def f_part1(hl, hg=hg, wl=wl):
                h = hg * 4 + hl
                bank = rr_bank(0, 4)
                fbank[hl] = bank
                proj_fm(bank, wl, hl, xnT, TT)
                par = hl % 2
                tU, U = tmp["U", par]; tL1, L1 = tmp["L1", par]; tL2, L2 = tmp["L2", par]
                zf = ps[:, bank, 0:TT]
                A("act", lambda e, zf=zf, U=U: e.activation(out=U[:, 0:TT], in_=zf, func=AF.Exp, scale=-1.0),
                  writes=[PB[bank], tU])
                A("act", lambda e, U=U, L1=L1: e.activation(out=L1[:, 0:TT], in_=U[:, 0:TT], func=AF.Ln, bias=onec[:, 0:1]),
                  reads=[tU, T_c], writes=[tL1])
                A("act", lambda e, U=U, L2=L2, h=h: e.activation(out=L2[:, 0:TT], in_=U[:, 0:TT], func=AF.Ln,
                                                                  scale=lbT[:, h:h + 1], bias=onec[:, 0:1]),
                  reads=[tU, T_c], writes=[tL2])
                A("dve", lambda e, L1=L1, L2=L2: e.tensor_tensor(out=L2[:, 0:TT], in0=L2[:, 0:TT], in1=L1[:, 0:TT], op=ALU.subtract),
                  reads=[tL1], writes=[tL2])
                A("dve", lambda e, L2=L2, hl=hl: e.tensor_tensor_scan(out=B4[:, hl, 0:TT], data0=rmask[:, 0:TT], data1=L2[:, 0:TT],
                                                                       initial=0.0, op0=ALU.mult, op1=ALU.add),
                  reads=[tL2, T_c], writes=[t_B4])
                A("dve", lambda e, L1=L1, hl=hl: e.tensor_tensor(out=L1[:, 0:TT], in0=L1[:, 0:TT], in1=B4[:, hl, 0:TT], op=ALU.add),
                  reads=[t_B4], writes=[tL1])
                A("dve", lambda e, zf=zf, L1=L1, U=U: e.tensor_tensor(out=U[:, 0:TT], in0=zf, in1=L1[:, 0:TT], op=ALU.add),
                  reads=[tL1], writes=[PB[bank], tU])

            def f_part2(hl, hg=hg):
                h = hg * 4 + hl
                tU, U = tmp["U", hl % 2]
                A("act", lambda e, U=U, h=h: e.activation(out=kt[:, h, 0:TT], in_=U[:, 0:TT], func=AF.Exp, scale=-1.0,
                                                           bias=lnoml[:, h:h + 1]),
                  reads=[tU, T_c], writes=[t_kt])
                A("act", lambda e, hl=hl, h=h: e.activation(
                    out=EL[:, h, 0:nchunk], in_=B4[:, hl, 0:TT].rearrange("p (c t) -> p c t", t=64)[:, :, 63], func=AF.Exp),
                  reads=[t_B4], writes=[T_EL[h]])

            for hl in range(4):
                f_part1(hl)
                if hl >= 1:
                    f_part2(hl - 1)
            f_part2(3)
            W.release(2)
            wl = [W.acquire(wsrc_in(hg * 512, 0)), W.acquire(wsrc_in(hg * 512, 1))]
            qbank = {}

            def q_part1(hl, hg=hg, wl=wl):
                bank = rr_bank(0, 4)
                qbank[hl] = bank
                proj_fm(bank, wl, hl, xnT, TT)
                par = hl % 2
                tW, Wq = tmp["Wq", par]; tL3, L3 = tmp["L3", par]
                zq = ps[:, bank, 0:TT]
                A("act", lambda e, zq=zq, Wq=Wq: e.activation(out=Wq[:, 0:TT], in_=zq, func=AF.Exp, scale=-1.0),
                  writes=[PB[bank], tW])
                A("act", lambda e, Wq=Wq, L3=L3: e.activation(out=L3[:, 0:TT], in_=Wq[:, 0:TT], func=AF.Ln, bias=onec[:, 0:1]),
                  reads=[tW, T_c], writes=[tL3])
                A("dve", lambda e, L3=L3, hl=hl: e.tensor_tensor(out=L3[:, 0:TT], in0=B4[:, hl, 0:TT], in1=L3[:, 0:TT], op=ALU.subtract),
                  reads=[t_B4], writes=[tL3])

            def q_part2(hl, hg=hg):
                h = hg * 4 + hl
                par = hl % 2
                bank = qbank[hl]
                tW, Wq = tmp["Wq", par]; tL3, L3 = tmp["L3", par]
                zq = ps[:, bank, 0:TT]
                A("act", lambda e, Wq=Wq, L3=L3: e.activation(out=Wq[:, 0:TT], in_=L3[:, 0:TT], func=AF.Exp),
                  reads=[tL3], writes=[tW])
                A("dve", lambda e, zq=zq, Wq=Wq, h=h: e.tensor_tensor(out=qt[:, h, 0:TT], in0=zq, in1=Wq[:, 0:TT], op=ALU.mult),
                  reads=[tW], writes=[PB[bank], t_qt])

            for hl in range(4):
                q_part1(hl)
                if hl >= 1:
                    q_part2(hl - 1)
            q_part2(3)
            W.release(2)
            wl = [W.acquire(wsrc_in(2048 + hg * 512, 0)), W.acquire(wsrc_in(2048 + hg * 512, 1))]
            for i, tl in enumerate(tiles):
                nt, col = tl["nt"], tl["col"]
                bank = rr_bank(4, 8)
                proj_tm(bank, wl, xnT, col, nt)
                tv, vap = vsb[i]
                A("act", lambda e, bank=bank, nt=nt, vap=vap, hg=hg: e.activation(
                    out=vap[0:nt, hg * 512:(hg + 1) * 512], in_=ps[0:nt, bank, :], func=AF.Copy),
                  writes=[PB[bank], tv])
            W.release(2)
            wl = [W.acquire(wsrc_in(3072 + hg * 512, 0)), W.acquire(wsrc_in(3072 + hg * 512, 1))]
            for i, tl in enumerate(tiles):
                nt, col = tl["nt"], tl["col"]
                bank = rr_bank(4, 8)
                proj_tm(bank, wl, xnT, col, nt)
                tg, gap = Gsb[i]
                tq_, sq_ = t_sqj1, sqj1
                A("act", lambda e, bank=bank, nt=nt, sq_=sq_: e.activation(out=sq_[0:nt, 0:512], in_=ps[0:nt, bank, :], func=AF.Silu),
                  writes=[PB[bank], tq_])
                A("dve", lambda e, nt=nt, sq_=sq_, gap=gap, hg=hg: e.tensor_tensor(
                    out=gap[0:nt, hg * 512:(hg + 1) * 512], in0=sq_[0:nt, 0:512], in1=gnorm_b[0:nt, hg * 512:(hg + 1) * 512], op=ALU.mult),
                  reads=[tq_, T_c], writes=[tg])
            W.release(2)

        t_qT, qT_raw = AR.take("qT", LOC + 40 * 1024, 8192)
        qT = qT_raw.bitcast(BF16).rearrange("p (c t) -> p c t", t=512)
        Gb = []
        for i in range(ntl):
            t1, a1 = AR.take("Gb%d" % i, LOC + 60 * 1024 + i * 2048, 2048)
            Gb.append((t1, a1.bitcast(BF16)))
        t_stg, stg = AR.take("stg", LOC + 68 * 1024, 2048)
        t_stg2, stg2 = AR.take("stg2", LOC + 70 * 1024, 2048)
        def kv_rows(tl):
            if tl["kind"] == "s":
                return ks[tl["s"]], vs[tl["s"]]
            if tl["gt"] in out_tiles:
                r0 = (tl["gt"] - (NPT - 4)) * 128
                return kp[r0:r0 + 128, :], vp[r0:r0 + 128, :]
            return None

        def slot_of(tl):
            return (tl["gt"] % 8) if tl["kind"] == "p" else tl["s"]

        ksegs = []
        for tl in tiles:
            if tl["kind"] == "p":
                kc = (tl["gt"] % 8) * 128
                hd = T_kT[tl["gt"] % 8]
            else:
                kc = tl["s"] * ST
                hd = T_kT[0]
            if ksegs and ksegs[-1][0] + ksegs[-1][1] == tl["col"] and ksegs[-1][2] + ksegs[-1][1] == kc:
                p0, n0_, k0_, hs_ = ksegs[-1]
                ksegs[-1] = (p0, n0_ + tl["nt"], k0_, hs_ + ([hd] if hd not in hs_ else []))
            else:
                ksegs.append((tl["col"], tl["nt"], kc, [hd]))

        def gen_A2():
            t_osb, osb = AR.take("osb", LOC + 50 * 1024, 4096)
            t_sqj, sqj = AR.take("sqj", LOC + 54 * 1024, 4096)
            t_og, og_raw = AR.take("og", LOC + 58 * 1024, 2048)
            og = og_raw.bitcast(BF16)
            t_ktok, ktok_raw = AR.take("ktok", B4_off, 2048)
            ktok = ktok_raw.bitcast(BF16).rearrange("p (h d) -> p h d", d=128)
            t_Asb, Asb_raw = AR.take("Asb", B4_off + 2048, 1024)
            Asb = Asb_raw.bitcast(BF16).rearrange("p (h t) -> p h t", t=64)
            t_s0, _r0 = AR.take("stmp_g0", B4_off + 3072, 2048)
            t_s1, _r1 = AR.take("stmp_g1", B4_off + 5120, 2048)
            t_stm = [t_s0, t_s1]
            stmp = AR.ap[:, (B4_off + 3072) // 4:(B4_off + 7168) // 4].rearrange("p (h d) -> p h d", d=128)
            cidx = 0
            for i, tl in enumerate(tiles):
                nt, col = tl["nt"], tl["col"]
                nch = nt // 64
                tv, vap = vsb[i]
                tg, gap = Gsb[i]
                is_samp = tl["kind"] == "s"
                if is_samp:
                    s = tl["s"]
                    S.dma("sp", lambda e, s=s: e.dma_start(out=S32[:], in_=st_in[s].rearrange("h k v -> k h v")),
                          writes=T_S32)
                    A("act", lambda e: e.activation(out=Sbf[:], in_=S32[:], func=AF.Copy), reads=T_S32, writes=T_Sbf)
                for h in range(8):
                    A("pe", lambda e, h=h, col=col, nt=nt: e.transpose(
                        out=ps[0:nt, 0, :].bitcast(BF16)[:, h * 128:(h + 1) * 128], in_=kt[:, h, col:col + nt], identity=ident_b[:]),
                      reads=[t_kt, T_c], writes=[PB[0]], inc=(h == 7))
                A("act", lambda e, nt=nt: e.activation(out=ktok[0:nt, :, :],
                                                       in_=ps[0:nt, 0, :].bitcast(BF16).rearrange("p (h d) -> p h d", d=128), func=AF.Copy),
                  writes=[PB[0], t_ktok])
                for h in range(8):
                    for cc in range(nch):
                        c0 = col + cc * 64
                        A("pe", lambda e, h=h, c0=c0, cc=cc: e.matmul(
                            ps[cc * 64:(cc + 1) * 64, 0, h * 64:(h + 1) * 64], lhsT=kt[:, h, c0:c0 + 64], rhs=qt[:, h, c0:c0 + 64],
                            start=True, stop=True),
                          reads=[t_kt, t_qt], writes=[PB[0]], inc=(h == 7 and cc == nch - 1))
                A("dve", lambda e, nt=nt: e.tensor_tensor(
                    out=Asb[0:nt, :, :], in0=ps[0:nt, 0, :].rearrange("p (h t) -> p h t", t=64),
                    in1=mask01[0:nt, :].unsqueeze(1).to_broadcast([nt, 8, 64]), op=ALU.mult),
                  reads=[T_c], writes=[PB[0], t_Asb])
                yield
                for cc in range(nch):
                    c0 = col + cc * 64
                    rows = slice(cc * 64, (cc + 1) * 64)
                    ch = cidx + cc
                    for h in range(8):
                        bo, oc = 1 + h // 4, (h % 4) * 128
                        A("pe", lambda e, rows=rows, bo=bo, oc=oc, h=h, vap=vap: e.matmul(
                            ps[rows, bo, oc:oc + 128], lhsT=Asb[rows, h, :], rhs=vap[rows, h * 128:(h + 1) * 128], start=True, stop=False),
                          reads=[t_Asb, tv], writes=[PB[bo]], inc=False)
                        A("pe", lambda e, rows=rows, bo=bo, oc=oc, h=h, c0=c0: e.matmul(
                            ps[rows, bo, oc:oc + 128], lhsT=qt[:, h, c0:c0 + 64], rhs=Sbf[:, h, :], start=False, stop=True),
                          reads=[t_qt, T_Sbf[h]], writes=[PB[bo]], inc=(h % 4 == 3))
                    for h in range(8):
                        bp, pc = 3 + h // 4, (h % 4) * 128
                        A("pe", lambda e, rows=rows, bp=bp, pc=pc, h=h, vap=vap: e.matmul(
                            ps[:, bp, pc:pc + 128], lhsT=ktok[rows, h, :], rhs=vap[rows, h * 128:(h + 1) * 128], start=True, stop=True),
                          reads=[t_ktok, tv], writes=[PB[bp]], inc=(h % 4 == 3))
                    for g in range(2):
                        bp = 3 + g
                        hs = slice(4 * g, 4 * g + 4)
                        A("dve", lambda e, bp=bp, hs=hs: e.tensor_tensor(
                            out=stmp[:, hs, :], in0=ps[:, bp, :].rearrange("p (h d) -> p h d", d=128), in1=S32[:, hs, :], op=ALU.add),
                          reads=T_S32[4 * g:4 * g + 4], writes=[PB[bp], t_stm[g]])
                        A("dve", lambda e, hs=hs, ch=ch: e.tensor_tensor(
                            out=S32[:, hs, :], in0=stmp[:, hs, :], in1=EL[:, hs, ch:ch + 1].to_broadcast([128, 4, 128]), op=ALU.mult),
                          reads=[t_stm[g]] + T_EL[4 * g:4 * g + 4], writes=T_S32[4 * g:4 * g + 4])
                        A("act", lambda e, hs=hs: e.activation(out=Sbf[:, hs, :], in_=S32[:, hs, :], func=AF.Copy),
                          reads=T_S32[4 * g:4 * g + 4], writes=T_Sbf[4 * g:4 * g + 4])
                    yield
                for g in range(2):
                    A("act", lambda e, g=g, nt=nt: e.activation(out=osb[0:nt, g * 512:(g + 1) * 512], in_=ps[0:nt, 1 + g, :], func=AF.Copy),
                      writes=[PB[1 + g], t_osb])
                cidx += nch
                if is_samp:
                    d = S.dma("sp", lambda e, s=tl["s"]: e.dma_start(out=ss_o[s].rearrange("h k v -> k h v"), in_=S32[:]),
                              reads=T_S32)
                elif tl["gt"] == NPT - 1:
                    d = S.dma("sp", lambda e: e.dma_start(out=sp_o.rearrange("h k v -> k h v"), in_=S32[:]),
                              reads=T_S32)
                A("act", lambda e, nt=nt: e.activation(out=sqj[0:nt, :], in_=osb[0:nt, :], func=AF.Square),
                  reads=[t_osb], writes=[t_sqj])
                A("dve", lambda e, nt=nt: e.tensor_reduce(out=stats[0:nt, 8:16], in_=sqj[0:nt, :].rearrange("p (h d) -> p h d", d=128),
                                                          op=ALU.add, axis=AX.X),
                  reads=[t_sqj], writes=[T_stats])
                A("act", lambda e, nt=nt: e.activation(out=stats[0:nt, 16:24], in_=stats[0:nt, 8:16], func=AF.Sqrt, scale=1.0 / 128,
                                                       bias=epsc[0:nt, :]),
                  reads=[T_c], writes=[T_stats])
                A("dve", lambda e, nt=nt: e.reciprocal(stats[0:nt, 24:32], stats[0:nt, 16:24]), writes=[T_stats])
                A("dve", lambda e, nt=nt: e.tensor_tensor(
                    out=osb[0:nt, :].rearrange("p (h d) -> p h d", d=128), in0=osb[0:nt, :].rearrange("p (h d) -> p h d", d=128),
                    in1=stats[0:nt, 24:32].unsqueeze(2).to_broadcast([nt, 8, 128]), op=ALU.mult),
                  reads=[T_stats], writes=[t_osb])
                A("dve", lambda e, nt=nt, gap=gap: e.tensor_tensor(out=og[0:nt, :], in0=osb[0:nt, :], in1=gap[0:nt, :], op=ALU.mult),
                  reads=[t_osb, tg], writes=[t_og])
                bt = 0
                for c in range(8):
                    A("pe", lambda e, c=c, bt=bt, nt=nt: e.transpose(
                        out=ps[:, bt, 0:512].bitcast(BF16)[:, c * 128:c * 128 + nt], in_=og[0:nt, c * 128:(c + 1) * 128],
                        identity=ident_b[0:nt, 0:nt]),
                      reads=[t_og, T_c], writes=[PB[bt]], inc=(c == 7))
                A("act", lambda e, bt=bt, nt=nt, col=col: e.activation(
                    out=oaT[:, :, col:col + nt], in_=ps[:, bt, 0:512].bitcast(BF16).rearrange("p (c t) -> p c t", t=128)[:, :, 0:nt],
                    func=AF.Copy),
                  writes=[PB[bt], t_oa])
                yield


        def gen_B1():
            for hg in range(2):
                wl = [W.acquire(wsrc_in(4096 + hg * 512, 0)), W.acquire(wsrc_in(4096 + hg * 512, 1))]
                for c in range(4):
                    cc = hg * 4 + c
                    bank = rr_bank(5, 8)
                    proj_fm(bank, wl, c, xnT, TT)
                    A("act", lambda e, bank=bank, cc=cc: e.activation(out=qT[:, cc, 0:TT], in_=ps[:, bank, 0:TT], func=AF.Copy, scale=0.125),
                      writes=[PB[bank], t_qT])
                    yield
                W.release(2)
                wl = [W.acquire(wsrc_in(5120 + hg * 512, 0)), W.acquire(wsrc_in(5120 + hg * 512, 1))]
                for c in range(4):
                    cc = hg * 4 + c
                    bank = rr_bank(5, 8)
                    proj_fm(bank, wl, c, xnT, TT)
                    for (p0, n_, k0_, hs_) in ksegs:
                        A("dve", lambda e, bank=bank, cc=cc, p0=p0, n_=n_, k0_=k0_: e.tensor_copy(
                            kT[:, cc, k0_:k0_ + n_], ps[:, bank, p0:p0 + n_]),
                          writes=[PB[bank]] + hs_)
                    yield
                for i, tl in enumerate(tiles):
                    dst = kv_rows(tl)
                    if dst is None:
                        continue
                    nt, col = tl["nt"], tl["col"]
                    bank = rr_bank(5, 8)
                    proj_tm(bank, wl, xnT, col, nt)
                    A("act", lambda e, bank=bank, nt=nt: e.activation(out=stg[0:nt, :], in_=ps[0:nt, bank, :], func=AF.Copy),
                      writes=[PB[bank], t_stg])
                    d = S.dma("sp", lambda e, nt=nt, dd=dst[0][:, hg * 512:(hg + 1) * 512]: e.dma_start(out=dd, in_=stg[0:nt, :]),
                              reads=[t_stg])
                    yield
                W.release(2)
                wl = [W.acquire(wsrc_in(6144 + hg * 512, 0)), W.acquire(wsrc_in(6144 + hg * 512, 1))]
                for i, tl in enumerate(tiles):
                    nt, col = tl["nt"], tl["col"]
                    bank = rr_bank(5, 8)
                    proj_tm(bank, wl, xnT, col, nt)
                    sl = slot_of(tl)
                    A("dve", lambda e, bank=bank, nt=nt, sl=sl, hg=hg: e.tensor_copy(
                        vaug[0:nt, sl, hg * 8:(hg + 1) * 8, 0:64], ps[0:nt, bank, :].rearrange("p (h d) -> p h d", d=64)),
                      writes=[PB[bank], T_va[sl]])
                    dst = kv_rows(tl)
                    if dst is not None:
                        A("act", lambda e, bank=bank, nt=nt: e.activation(out=stg2[0:nt, :], in_=ps[0:nt, bank, :], func=AF.Copy),
                          writes=[PB[bank], t_stg2])
                        d = S.dma("sp", lambda e, nt=nt, dd=dst[1][:, hg * 512:(hg + 1) * 512]: e.dma_start(out=dd, in_=stg2[0:nt, :]),
                                  reads=[t_stg2])
                    yield
                W.release(2)
                wl = [W.acquire(wsrc_in(7168 + hg * 512, 0)), W.acquire(wsrc_in(7168 + hg * 512, 1))]
                for i, tl in enumerate(tiles):
                    nt, col = tl["nt"], tl["col"]
                    bank = rr_bank(5, 8)
                    proj_tm(bank, wl, xnT, col, nt)
                    tg, gap = Gb[i]
                    A("act", lambda e, bank=bank, nt=nt, gap=gap, hg=hg: e.activation(
                        out=gap[0:nt, hg * 512:(hg + 1) * 512], in_=ps[0:nt, bank, :], func=AF.Silu),
                      writes=[PB[bank], tg])
                    yield
                W.release(2)


        if pi == 0:
            table_compute()
        gA, gB = gen_A2(), gen_B1()
        doneA = doneB = False
        while not (doneA and doneB):
            if not doneA:
                try:
                    next(gA)
                except StopIteration:
                    doneA = True
            for _ in range(2):
                if not doneB:
                    try:
                        next(gB)
                    except StopIteration:
                        doneB = True

        off = LOC
        Pex = []
        for j in range(2):
            t1, a1 = AR.take("Pex%d" % j, off, 2560); off += 2560
            Pex.append((t1, a1))
        PTt = []
        for j in range(2):
            t1, a1 = AR.take("PT%d" % j, off, 1280); off += 1280
            PTt.append((t1, a1.bitcast(BF16)))
        t_on, on_ap = AR.take("on", off, 4096); off += 4096
        t_ogb, ogb_raw = AR.take("ogb", off, 2048); off += 2048
        ogb = ogb_raw.bitcast(BF16)
        t_kc, kc_raw = AR.take("kctok", off, 8192); off += 8192
        kctok = kc_raw.bitcast(BF16).rearrange("p (r c) -> p r c", c=1024)
        assert off <= LOC + 32 * 1024

        for i, tl in enumerate(tiles):
            nt, col = tl["nt"], tl["col"]
            tg, gap = Gb[i]
            is_samp = tl["kind"] == "s"
            if is_samp:
                s = tl["s"]
                S.dma("pool", lambda e, s=s: e.dma_start(out=kctok, in_=ck[s].rearrange("(r p) c -> p r c", p=128)),
                      writes=[t_kc])
                for r in range(4):
                    S.dma("pool", lambda e, s=s, r=r: e.dma_start(
                        out=vaug[:, 2 + r, :, 0:64], in_=cv[s, r * 128:(r + 1) * 128, :].rearrange("p (h d) -> p h d", d=64)),
                          writes=[T_va[2 + r]])
                for r in range(4):
                    bt = 6 + (r % 2)
                    for cc in range(8):
                        A("pe", lambda e, r=r, cc=cc, bt=bt: e.transpose(
                            out=ps[:, bt, 0:512].bitcast(BF16)[:, cc * 128:(cc + 1) * 128], in_=kctok[:, r, cc * 128:(cc + 1) * 128],
                            identity=ident_b[:]),
                          reads=[t_kc, T_c], writes=[PB[bt]], inc=(cc == 7))
                    A("dve", lambda e, r=r, bt=bt: e.tensor_copy(
                        kT[:, :, (1 + r) * 128:(2 + r) * 128], ps[:, bt, 0:512].bitcast(BF16).rearrange("p (c t) -> p c t", t=128)),
                      writes=[PB[bt], T_kT[1 + r]])
                blocks = [((1 + r) * 128, 128, 2 + r, (0 if r == 3 else None), "full", 1 + r) for r in range(4)]
                blocks.append((s * ST, 64, s, 1, "own", 0))
            else:
                gt = tl["gt"]
                blocks = []
                for r in range(5):
                    j = gt - 4 + r
                    if j < 0:
                        continue
                    sl = j % 8
                    kind = "first" if r == 0 else ("own" if r == 4 else "full")
                    blocks.append((sl * 128, 128, sl, {3: 0, 4: 1}.get(r), kind, sl))
            nb = len(blocks)

            def emit_ST(h, blocks=blocks, nb=nb, nt=nt, col=col):
                c, pb = h // 2, (h % 2) * 64
                b0 = 2 * (h % 2)
                for bi, (kc0, nk, sl, tb, kind, kh) in enumerate(blocks):
                    bank = b0 if bi < 4 else b0 + 1
                    cb = (bi % 4) * 128
                    krd = [T_kT[kh]]
                    A("pe", lambda e, bank=bank, cb=cb, nk=nk, kc0=kc0, pb=pb, c=c, col=col, nt=nt: e.matmul(
                        ps[0:nk, bank, cb:cb + nt], lhsT=kT[pb:pb + 64, c, kc0:kc0 + nk], rhs=qT[pb:pb + 64, c, col:col + nt],
                        start=True, stop=True),
                      reads=krd + [t_qT], writes=[PB[bank]], inc=(bi == nb - 1 or bi == 3))

            def emit_soft(h, blocks=blocks, nb=nb, nt=nt):
                b0 = 2 * (h % 2)
                tP, Pe = Pex[h % 2]
                tPT, PT = PTt[h % 2]
                ntab = sum(1 for bl in blocks if bl[3] is not None)
                nplain = nb - ntab
                if nplain > 0:
                    A("act", lambda e, b0=b0, nplain=nplain, PT=PT, nt=nt: e.activation(
                        out=PT[:, 0:nplain * 128].rearrange("p (b t) -> p b t", t=128)[:, :, 0:nt],
                        in_=ps[:, b0, 0:nplain * 128].rearrange("p (b t) -> p b t", t=128)[:, :, 0:nt], func=AF.Exp),
                      writes=[PB[b0], tPT])
                if nt == 128 and blocks[0][4] == "first":
                    A("dve", lambda e, PT=PT: e.memset(PT[0:64, 64:128], 0.0), writes=[tPT])
                for bi, (kc0, nk, sl, tb, kind, kh) in enumerate(blocks):
                    if tb is None:
                        continue
                    bank = b0 if bi < 4 else b0 + 1
                    cbp = (bi % 4) * 128
                    cb = bi * 128
                    A("act", lambda e, bank=bank, cbp=cbp, cb=cb, nk=nk, nt=nt, Pe=Pe: e.activation(
                        out=Pe[0:nk, cb:cb + nt], in_=ps[0:nk, bank, cbp:cbp + nt], func=AF.Exp),
                      writes=[PB[bank], tP])
                tabs = [(bi, bl) for bi, bl in enumerate(blocks) if bl[3] is not None]
                if len(tabs) == 2 and nt == 128 and tabs[0][1][1] == 128 and tabs[1][1][1] == 128 and tabs[0][1][3] == 0:
                    cb = tabs[0][0] * 128
                    A("dve", lambda e, PT=PT, Pe=Pe, cb=cb, h=h: e.tensor_tensor(
                        out=PT[:, cb:cb + 256], in0=Pe[:, cb:cb + 256], in1=tabE[:, h, 0:256], op=ALU.mult),
                      reads=[tP, T_tab], writes=[tPT])
                else:
                    for bi, (kc0, nk, sl, tb, kind, kh) in tabs:
                        cb = bi * 128
                        A("dve", lambda e, PT=PT, Pe=Pe, cb=cb, nt=nt, nk=nk, tb=tb, h=h: e.tensor_tensor(
                            out=PT[0:nk, cb:cb + nt], in0=Pe[0:nk, cb:cb + nt], in1=tabE[0:nk, h, tb * 128:tb * 128 + nt], op=ALU.mult),
                          reads=[tP, T_tab], writes=[tPT])

            def emit_PV(h, blocks=blocks, nb=nb, nt=nt, gap=gap, tg=tg):
                tPT, PT = PTt[h % 2]
                bpv = 4 + (h // 7)
                hh = h % 7
                O = lambda lo, hi, bpv=bpv, hh=hh: ps[lo:hi, bpv, hh * 65:hh * 65 + 65]
                mm = []
                for bi, (kc0, nk, sl, tb, kind, kh) in enumerate(blocks):
                    cb = bi * 128
                    if kind == "full" or (nt == 128 and nk == 128):
                        mm.insert(0, (0, nt, PT[0:128, cb:cb + nt], vaug[0:128, sl, h, :], [T_va[sl]]))
                for bi, (kc0, nk, sl, tb, kind, kh) in enumerate(blocks):
                    cb = bi * 128
                    if nt == 128 and nk == 128:
                        continue
                    if kind == "first":
                        mm.append((0, 64, PT[0:128, cb:cb + 64], vaug[0:128, sl, h, :], [T_va[sl]]))
                        mm.append((64, 128, PT[64:128, cb + 64:cb + 128], vaug[64:128, sl, h, :], [T_va[sl]]))
                    elif kind == "own":
                        if nt == 128:
                            mm.append((0, 64, PT[0:64, cb:cb + 64], vaug[0:64, sl, h, :], [T_va[sl]]))
                            mm.append((64, 128, PT[0:128, cb + 64:cb + 128], vaug[0:128, sl, h, :], [T_va[sl]]))
                        else:
                            mm.append((0, 64, PT[0:64, cb:cb + 64], vaug[0:64, sl, h, :], [T_va[sl]]))
                covered = [False, False]
                hv = [([0, 1] if (lo == 0 and hi == 128) else ([0] if lo == 0 else [1])) for (lo, hi, _, _, _) in mm]
                last_of = {}
                for mi, halves in enumerate(hv):
                    for x in halves:
                        last_of[x] = mi
                for mi, (lo, hi, l_ap, r_ap, rds) in enumerate(mm):
                    halves = hv[mi]
                    st_ = not all(covered[x] for x in halves)
                    for x in halves:
                        covered[x] = True
                    last = (mi == len(mm) - 1)
                    sp_ = any(last_of[x] == mi for x in halves)
                    A("pe", lambda e, lo=lo, hi=hi, l_ap=l_ap, r_ap=r_ap, st_=st_, sp_=sp_, O=O: e.matmul(
                        O(lo, hi), lhsT=l_ap, rhs=r_ap, start=st_, stop=sp_),
                      reads=[tPT] + rds, writes=[PB[bpv]], inc=last)
                if not (hh == 6 or h == 15):
                    return None

                def norm(h=h, hh=hh, bpv=bpv, nt=nt, gap=gap, tg=tg):
                    nh = hh + 1
                    h0 = h - hh
                    pv = ps[0:nt, bpv, 0:nh * 65].rearrange("p (h d) -> p h d", d=65)
                    A("dve", lambda e, pv=pv, nt=nt, nh=nh: e.reciprocal(stats[0:nt, 32:32 + nh], pv[:, :, 64]),
                      writes=[PB[bpv], T_stats])
                    onv = on_ap[0:nt, h0 * 64:(h0 + nh) * 64].rearrange("p (h d) -> p h d", d=64)
                    A("dve", lambda e, pv=pv, nt=nt, nh=nh, onv=onv: e.tensor_tensor(
                        out=onv, in0=pv[:, :, 0:64], in1=stats[0:nt, 32:32 + nh].unsqueeze(2).to_broadcast([nt, nh, 64]), op=ALU.mult),
                      reads=[T_stats], writes=[PB[bpv], t_on])
                    A("dve", lambda e, nt=nt, nh=nh, h0=h0, gap=gap: e.tensor_tensor(
                        out=ogb[0:nt, h0 * 64:(h0 + nh) * 64], in0=on_ap[0:nt, h0 * 64:(h0 + nh) * 64],
                        in1=gap[0:nt, h0 * 64:(h0 + nh) * 64], op=ALU.mult),
                      reads=[t_on, tg], writes=[t_ogb])
                return norm

            emit_ST(0)
            pending = None
            for h in range(16):
                if h + 1 < 16:
                    emit_ST(h + 1)
                emit_soft(h)
                if pending is not None:
                    pending()
                    pending = None
                pending = emit_PV(h)
            if pending is not None:
                pending()
            bt = 7
            for c in range(8):
                A("pe", lambda e, c=c, bt=bt, nt=nt: e.transpose(
                    out=ps[:, bt, 0:512].bitcast(BF16)[:, c * 128:c * 128 + nt], in_=ogb[0:nt, c * 128:(c + 1) * 128],
                    identity=ident_b[0:nt, 0:nt]),
                  reads=[t_ogb, T_c], writes=[PB[bt]], inc=(c == 7))
            A("act", lambda e, bt=bt, nt=nt, col=col: e.activation(
                out=obT[:, :, col:col + nt], in_=ps[:, bt, 0:512].bitcast(BF16).rearrange("p (c t) -> p c t", t=128)[:, :, 0:nt],
                func=AF.Copy),
              writes=[PB[bt], t_ob])

        off = LOC
        t_sa, sa_raw = AR.take("sa", off, 8192); off += 8192
        sa = sa_raw.rearrange("p (c t) -> p c t", t=512)
        t_sb_, sb_raw = AR.take("sbg", off, 8192); off += 8192
        sbg = sb_raw.rearrange("p (c t) -> p c t", t=512)
        t_mT, mT_raw = AR.take("mT", off, 16384); off += 16384
        mT = mT_raw.bitcast(BF16).rearrange("p (k t) -> p k t", t=512)
        xr = []
        for i in range(ntl):
            t1, a1 = AR.take("xr%d" % i, off, 8192); off += 8192
            xr.append((t1, a1))
        assert off <= ARENA_BYTES
        for i, tl in enumerate(tiles):
            nt = tl["nt"]
            t1, a1 = xr[i]
            S.dma("sp", lambda e, d=a1[0:nt, :], s=xsrc(tl): e.dma_start(out=d, in_=s), writes=[t1])

        nxt_tiles = passes[pi + 1] if pi + 1 < len(passes) else None

        def prefetch_x(ids):
            for i2 in ids:
                if nxt_tiles is None or i2 >= len(nxt_tiles):
                    continue
                tl2 = nxt_tiles[i2]
                t_x2, x_ap2 = AR.take("x%d" % i2, LOC + i2 * 8192, 8192)
                S.dma("sp", lambda e, d=x_ap2[0:tl2["nt"], :], s=xsrc(tl2): e.dma_start(out=d, in_=s), writes=[t_x2])
                PREF[i2] = (t_x2, x_ap2)

        for cg in range(4):
            wl = [W.acquire(wsrc_in(8192 + cg * 512, 0)), W.acquire(wsrc_in(8192 + cg * 512, 1))]
            for c in range(4):
                bank = rr_bank(0, 4)
                proj_fm(bank, wl, c, xnT, TT)
                A("act", lambda e, bank=bank, c=c: e.activation(out=sa[:, c, 0:TT], in_=ps[:, bank, 0:TT], func=AF.Sigmoid),
                  writes=[PB[bank], t_sa])
            W.release(2)
            wl = [W.acquire(w_pa_v[:, :, cg * 512:(cg + 1) * 512])]
            rd_T = t_oa
            for c in range(4):
                bank = rr_bank(4, 8)
                proj_fm(bank, wl, c, oaT, TT)
                A("dve", lambda e, bank=bank, c=c: e.tensor_tensor(out=sa[:, c, 0:TT], in0=ps[:, bank, 0:TT], in1=sa[:, c, 0:TT], op=ALU.mult),
                  writes=[PB[bank], t_sa])
            W.release(1)
            rd_T = T_xn[0]
            wl = [W.acquire(wsrc_in(10240 + cg * 512, 0)), W.acquire(wsrc_in(10240 + cg * 512, 1))]
            for c in range(4):
                bank = rr_bank(0, 4)
                proj_fm(bank, wl, c, xnT, TT)
                A("act", lambda e, bank=bank, c=c: e.activation(out=sbg[:, c, 0:TT], in_=ps[:, bank, 0:TT], func=AF.Sigmoid),
                  writes=[PB[bank], t_sb_])
            W.release(2)
            wl = [W.acquire(w_pb_v[:, :, cg * 512:(cg + 1) * 512])]
            rd_T = t_ob
            for c in range(4):
                bank = rr_bank(4, 8)
                proj_fm(bank, wl, c, obT, TT)
                A("dve", lambda e, bank=bank, c=c: e.tensor_tensor(out=sbg[:, c, 0:TT], in0=ps[:, bank, 0:TT], in1=sbg[:, c, 0:TT], op=ALU.mult),
                  writes=[PB[bank], t_sb_])
                A("dve", lambda e, c=c, cg=cg: e.tensor_tensor(out=mT[:, cg * 4 + c, 0:TT], in0=sbg[:, c, 0:TT], in1=sa[:, c, 0:TT], op=ALU.add),
                  reads=[t_sa, t_sb_], writes=[t_mT])
            W.release(1)
            rd_T = T_xn[0]

        prefetch_x([0, 1])

        ysb = []
        for i in range(ntl):
            t1, a1 = AR.take("ysb%d" % i, i * 8192, 8192)
            ysb.append((t1, a1))

        def final(i, tl):
            nt = tl["nt"]
            t1, y_ap = ysb[i]
            tx, x_ap = xr[i]
            A("act", lambda e, nt=nt, y_ap=y_ap: e.activation(out=junkS[0:nt, :], in_=y_ap[0:nt, :], func=AF.Square, accum_out=stats[0:nt, 40:41]),
              reads=[t1], writes=[T_junk, T_stats])
            A("act", lambda e, nt=nt: e.activation(out=stats[0:nt, 41:42], in_=stats[0:nt, 40:41], func=AF.Sqrt, scale=1.0 / D, bias=epsc[0:nt, :]),
              reads=[T_c], writes=[T_stats])
            A("dve", lambda e, nt=nt: e.reciprocal(stats[0:nt, 42:43], stats[0:nt, 41:42]), writes=[T_stats])
            A("dve", lambda e, nt=nt, y_ap=y_ap: e.scalar_tensor_tensor(out=y_ap[0:nt, :], in0=y_ap[0:nt, :], scalar=stats[0:nt, 42:43],
                                                                        op0=ALU.mult, in1=gpost_b[0:nt, :], op1=ALU.mult),
              reads=[T_stats, T_c], writes=[t1])
            A("dve", lambda e, nt=nt, y_ap=y_ap, x_ap=x_ap: e.tensor_tensor(out=y_ap[0:nt, :], in0=y_ap[0:nt, :], in1=x_ap[0:nt, :], op=ALU.add),
              reads=[tx], writes=[t1])
            S.dma("sp", lambda e, nt=nt, y_ap=y_ap, d=ydst(tl): e.dma_start(out=d, in_=y_ap[0:nt, :]), reads=[t1])

        rd_T = t_mT
        for n in range(4):
            wl = [W.acquire(w_out_v[:, 0:8, n * 512:(n + 1) * 512]), W.acquire(w_out_v[:, 8:16, n * 512:(n + 1) * 512])]
            for i, tl in enumerate(tiles):
                nt, col = tl["nt"], tl["col"]
                bank = rr_bank(0, 8)
                proj_tm(bank, wl, mT, col, nt)
                t1, a1 = ysb[i]
                A("act", lambda e, bank=bank, nt=nt, a1=a1, n=n: e.activation(out=a1[0:nt, n * 512:(n + 1) * 512], in_=ps[0:nt, bank, :], func=AF.Copy),
                  writes=[PB[bank], t1])
                if n == 3:
                    final(i, tl)
            W.release(2)
        rd_T = T_xn[0]
        prefetch_x([2, 3])

    table_dma()
    W.start()
    for pi_, tiles_ in enumerate(passes):
        do_pass(pi_, tiles_)

    S.wait_all("sp", [(k, v) for k, v in S.dma_sems.items() if v > 0])
    return W


def build_nc(cfg):
    nc0 = bass.Bass("TRN2", target_bir_lowering=False)
    with ExitStack() as st0:
        W0 = _emit_program(nc0, Sched(), st0, None, cfg)
        plan_shapes = [(src.offset, tuple(map(tuple, src.ap)), src.tensor.name) for src in W0.rec]
    nc = bass.Bass("TRN2", target_bir_lowering=False)
    with ExitStack() as st:
        S = Sched()
        plan = _PlanProxy(plan_shapes)
        W = _emit_program(nc, S, st, plan, cfg)
        assert W.n_acq == len(plan_shapes), (W.n_acq, len(plan_shapes))
        S.emit(nc, st)
        cfg["_stats"] = dict(cnt=dict(S.cnt), dma=dict(S.dma_sems), nw=len(plan_shapes))
    return nc


class _PlanProxy:
    def __init__(self, shapes):
        self.shapes = shapes
        self.tensors = {}

    def bind(self, ap):
        self.tensors[ap.tensor.name] = ap.tensor

    def __len__(self):
        return len(self.shapes)

    def __getitem__(self, i):
        off, ap, name = self.shapes[i]
        t = self.tensors[name]
        return bass.AP(tensor=t, offset=off, ap=[list(x) for x in ap])

    def scratch(self, i):
        off, ap, name = self.shapes[i]
        t = self.tensors[name + "_bf"]
        return bass.AP(tensor=t, offset=off, ap=[list(x) for x in ap])


_NC_CACHE = {}


def _get_nc(cfg_key, cfg):
    if cfg_key not in _NC_CACHE:
        _NC_CACHE[cfg_key] = build_nc(cfg)
    return _NC_CACHE[cfg_key]


def kernel(x_prompt, x_sample, state_hgrn, cache_k, cache_v, norm_pre, w_in, lb_logits, gnorm_a, rel_bias,
           w_proj_a, w_proj_b, w_out, norm_post, _cfg=None):
    cfg = dict(n_prompt_tiles=SEQ // 128)
    if _cfg:
        cfg.update(_cfg)
    NPT = cfg["n_prompt_tiles"]
    f = lambda a: np.ascontiguousarray(np.asarray(a, dtype=np.float32))
    x_prompt, x_sample = f(x_prompt), f(x_sample)
    state_hgrn, cache_k, cache_v = f(state_hgrn), f(cache_k), f(cache_v)
    shared = dict(
        w_in=f(w_in[0]), w_pa=f(w_proj_a[0]), w_pb=f(w_proj_b[0]), w_out=f(w_out[0]),
        norm_pre=f(norm_pre[0]).reshape(1, D), norm_post=f(norm_post[0]).reshape(1, D),
        gnorm=f(gnorm_a[0]).reshape(1, DA), lb_logits=f(lb_logits), rel_bias=f(rel_bias[0]),
    )
    in_maps = []
    for c in range(NCORES):
        m = dict(shared)
        m["xp"] = np.ascontiguousarray(x_prompt[c, :NPT * 128])
        m["xs"] = np.ascontiguousarray(x_sample[2 * c:2 * c + 2])
        m["st_in"] = np.ascontiguousarray(state_hgrn[0, 2 * c:2 * c + 2])
        m["ck"] = np.ascontiguousarray(cache_k[0, 2 * c:2 * c + 2].reshape(2, 512, 1024))
        m["cv"] = np.ascontiguousarray(cache_v[0, 2 * c:2 * c + 2].reshape(2, 512, 1024))
        in_maps.append(m)
    nc = _get_nc((NPT, tuple(cfg.get('sizes') or ())), cfg)
    res = run_bass_kernel_spmd(nc, in_maps, core_ids=list(range(NCORES)))
    R = res.results
    y_prompt = np.stack([R[c]["yp"] for c in range(NCORES)], 0)
    y_sample = np.concatenate([R[c]["ys"] for c in range(NCORES)], 0)
    nsp = np.stack([R[c]["sp_o"] for c in range(NCORES)], 0)[None]
    nkp = np.stack([R[c]["kp"].reshape(512, 16, 64) for c in range(NCORES)], 0)[None]
    nvp = np.stack([R[c]["vp"].reshape(512, 16, 64) for c in range(NCORES)], 0)[None]
    nss = np.concatenate([R[c]["ss_o"] for c in range(NCORES)], 0)[None]
    nks = np.concatenate([R[c]["ks"].reshape(2, ST, 16, 64) for c in range(NCORES)], 0)[None]
    nvs = np.concatenate([R[c]["vs"].reshape(2, ST, 16, 64) for c in range(NCORES)], 0)[None]
    return (y_prompt, y_sample, nsp, nkp, nvp, nss, nks, nvs)
```

```python
import numpy as np
from contextlib import ExitStack
import concourse.bass as bass
import concourse.mybir as mybir
from concourse.bass_utils import run_bass_kernel_spmd

F32 = mybir.dt.float32
BF16 = mybir.dt.bfloat16
AF = mybir.ActivationFunctionType
ALU = mybir.AluOpType
AX = mybir.AxisListType

D = 2048
KC = 16
DA = 1024
EPS = 1e-6
NCORES = 8
SEQ = 4096
NSAMP = 2
ST = 64

SAME_ENGINE_SYNC = True
RELAX_SAME_ENGINE = False


class T:
    __slots__ = ("name", "w", "r")

    def __init__(self, name):
        self.name = name
        self.w = None
        self.r = {}


class Sched:
    ENGS = ("pe", "act", "dve", "pool", "sp")

    def __init__(self):
        self.prog = {e: [] for e in self.ENGS}
        self.cnt = {e: 0 for e in self.ENGS}
        self.seen = {e: {} for e in self.ENGS}
        self.dma_sems = {}
        self.key_sems = {}

    def new_dma_sem(self):
        k = "d%d" % len(self.dma_sems)
        self.dma_sems[k] = 0
        return k

    def _need(self, eng, k, v, waits):
        if k == eng and (eng == "pe" or not SAME_ENGINE_SYNC):
            return
        if RELAX_SAME_ENGINE and k == eng and eng in ("act", "dve") and v < self.cnt[eng]:
            return
        if self.seen[eng].get(k, 0) >= v:
            return
        if waits.get(k, 0) < v:
            waits[k] = v

    def _deps(self, eng, reads, writes):
        waits = {}
        for t in reads:
            if t.w is not None:
                self._need(eng, t.w[0], t.w[1], waits)
        for t in writes:
            if t.w is not None:
                self._need(eng, t.w[0], t.w[1], waits)
            for k, v in t.r.items():
                self._need(eng, k, v, waits)
        for k, v in waits.items():
            self.seen[eng][k] = v
        return list(waits.items())

    def _record(self, me, reads, writes):
        k, v = me
        for t in reads:
            if t.r.get(k, 0) < v:
                t.r[k] = v
        for t in writes:
            t.w = me
            t.r = {}

    def op(self, eng, fn, reads=(), writes=(), inc=True):
        waits = self._deps(eng, reads, writes)
        if inc:
            self.cnt[eng] += 1
            me = (eng, self.cnt[eng])
        else:
            me = (eng, self.cnt[eng] + 1)
        self.prog[eng].append((waits, fn, (eng, 1) if inc else None))
        self._record(me, reads, writes)
        return me

    def dma(self, q, fn, reads=(), writes=(), sem=None, key=None):
        if sem is None:
            if key is None:
                key = (writes[0] if len(writes) else reads[0]).name
            if key not in self.key_sems:
                self.key_sems[key] = self.new_dma_sem()
            sem = self.key_sems[key]
        waits = self._deps(q, reads, writes)
        self.dma_sems[sem] += 16
        me = (sem, self.dma_sems[sem])
        self.prog[q].append((waits, fn, (sem, 16)))
        self._record(me, reads, writes)
        return me

    def wait_all(self, eng, deps):
        waits = {}
        for k, v in deps:
            if self.seen[eng].get(k, 0) < v and waits.get(k, 0) < v:
                waits[k] = v
        for k, v in waits.items():
            self.seen[eng][k] = v
        if waits:
            self.prog[eng].append((list(waits.items()), None, None))

    def emit(self, nc, stack):
        sems = {}
        for e in self.ENGS:
            sems[e] = stack.enter_context(nc.semaphore("s_" + e))
        for k in self.dma_sems:
            sems[k] = stack.enter_context(nc.semaphore("s_" + k))
        block = stack.enter_context(nc.Block())
        progs = self.prog

        def run(engh, lst):
            for waits, fn, inc in lst:
                for k, v in waits:
                    engh.wait_ge(sems[k], v)
                if fn is not None:
                    ins = fn(engh)
                    if inc is not None:
                        ins.then_inc(sems[inc[0]], inc[1])

        @block.tensor
        def _(e):
            run(e, progs["pe"])

        @block.scalar
        def _(e):
            run(e, progs["act"])

        @block.vector
        def _(e):
            run(e, progs["dve"])

        @block.gpsimd
        def _(e):
            run(e, progs["pool"])

        @block.sync
        def _(e):
            run(e, progs["sp"])


class Arena:
    def __init__(self, ap):
        self.ap = ap
        self.live = []

    def take(self, name, off, nbytes):
        assert off % 4 == 0 and nbytes % 4 == 0
        assert off + nbytes <= self.ap.shape[1] * 4, (name, off, nbytes)
        t = T(name)
        for (s, e, old) in self.live:
            if s < off + nbytes and off < e:
                if old.w is not None:
                    k, v = old.w
                    if t.r.get(k, 0) < v:
                        t.r[k] = v
                for k, v in old.r.items():
                    if t.r.get(k, 0) < v:
                        t.r[k] = v
        self.live = [(s, e, o) for (s, e, o) in self.live if not (s >= off and e <= off + nbytes)]
        self.live.append((off, off + nbytes, t))
        return t, self.ap[:, off // 4:(off + nbytes) // 4]


class WStream:
    def __init__(self, S, slots, plan):
        self.S = S
        self.slots = slots
        self.plan = plan
        self.rec = []
        self.n_acq = 0
        self.n_issued = 0
        self.n_rel = 0
        self.n_req = None
        self.tscr = []

    def _issue(self):
        if self.plan is None or self.n_issued >= len(self.plan):
            return
        idx = self.n_issued
        src = self.plan[idx]
        si = idx % len(self.slots)
        ap, t, sem = self.slots[si]
        nk, ncol = src.shape[1], src.shape[2]
        dst = ap[:, 0:nk, 0:ncol]
        if self.n_req is None:
            self.S.dma("pool", lambda e, dst=dst, src=src: e.dma_start(out=dst, in_=src), writes=[t], sem=sem)
        else:
            j = idx % self.n_req
            scr = self.plan.scratch(idx)
            if idx < self.n_req:
                self.S.dma("pool", lambda e, dst=dst, src=src: e.dma_start(out=dst, in_=src), writes=[t], sem=sem)
                self.S.dma("sp", lambda e, dst=dst, scr=scr: e.dma_start(out=scr, in_=dst), reads=[t], writes=[self.tscr[j]],
                           key="scrw%d" % si)
            else:
                self.S.dma("pool", lambda e, dst=dst, scr=scr: e.dma_start(out=dst, in_=scr), reads=[self.tscr[j]], writes=[t],
                           sem=sem)
        self.n_issued += 1

    def start(self):
        for _ in range(len(self.slots)):
            self._issue()

    def acquire(self, src):
        self.rec.append(src)
        ap, t, sem = self.slots[self.n_acq % len(self.slots)]
        self.n_acq += 1
        return ap, t

    def release(self, n=1):
        for _ in range(n):
            self.n_rel += 1
            self._issue()


def _emit_program(nc, S, st, plan, cfg):
    NPT = cfg["n_prompt_tiles"]
    dbg = cfg.get("debug", False)

    def din(name, shape):
        return nc.dram_tensor(name, shape, F32, kind="ExternalInput").ap()

    def dout(name, shape):
        return nc.dram_tensor(name, shape, F32, kind="ExternalOutput").ap()

    xp = din("xp", [NPT * 128, D])
    xs = din("xs", [NSAMP, ST, D])
    st_in = din("st_in", [NSAMP, 8, 128, 128])
    ck = din("ck", [NSAMP, 512, 1024])
    cv = din("cv", [NSAMP, 512, 1024])
    w_in = din("w_in", [D, 12288])
    w_pa = din("w_pa", [DA, D])
    w_pb = din("w_pb", [DA, D])
    w_out = din("w_out", [D, D])
    norm_pre = din("norm_pre", [1, D])
    norm_post = din("norm_post", [1, D])
    gnorm = din("gnorm", [1, DA])
    lb_logits = din("lb_logits", [2, DA])
    rel_bias = din("rel_bias", [16, 257])

    yp = dout("yp", [NPT * 128, D])
    ys = dout("ys", [NSAMP, ST, D])
    sp_o = dout("sp_o", [8, 128, 128])
    kp = dout("kp", [512, 1024])
    vp = dout("vp", [512, 1024])
    ss_o = dout("ss_o", [NSAMP, 8, 128, 128])
    ks = dout("ks", [NSAMP, ST, 1024])
    vs = dout("vs", [NSAMP, ST, 1024])
    rb_ext = nc.dram_tensor("rb_ext", [16, 384], F32, kind="Internal").ap()
    w_in_bf = nc.dram_tensor("w_in_bf", [D, 12288], BF16, kind="Internal").ap()
    w_pa_bf = nc.dram_tensor("w_pa_bf", [DA, D], BF16, kind="Internal").ap()
    w_pb_bf = nc.dram_tensor("w_pb_bf", [DA, D], BF16, kind="Internal").ap()
    w_out_bf = nc.dram_tensor("w_out_bf", [D, D], BF16, kind="Internal").ap()

    w_in_v = w_in.rearrange("(k p) n -> p k n", p=128)
    w_pa_v = w_pa.rearrange("(k p) n -> p k n", p=128)
    w_pb_v = w_pb.rearrange("(k p) n -> p k n", p=128)
    w_out_v = w_out.rearrange("(k p) n -> p k n", p=128)

    def sb(name, shape, dt=F32):
        return st.enter_context(nc.sbuf_tensor(name, shape, dt))

    NW = cfg.get("nw", 4)
    wslots = []
    for i in range(NW):
        ap = sb("wslot%d" % i, [128, 8, 512], BF16)
        wslots.append((ap, T("wslot%d" % i), S.new_dma_sem()))
    W = WStream(S, wslots, plan)

    ARENA_BYTES = 104 * 1024
    arena_ap = sb("arena", [128, ARENA_BYTES // 4], F32)
    AR = Arena(arena_ap)
    LOC = 32 * 1024
    _t0, _a0 = AR.take("cst16", LOC + 48 * 1024, 1536)
    cst16 = _a0[0:16, :]
    _t1, _a1 = AR.take("rbsb", LOC + 50 * 1024, 1536)
    rbsb = _a1[0:16, :]

    kT = sb("kT", [128, 8, 1024], BF16)
    vaug = sb("vaug", [128, 8, 16, 65], BF16)
    S32 = sb("S32", [128, 8, 128], F32)
    Sbf = sb("Sbf", [128, 8, 128], BF16)
    gpost_b = sb("gpost_b", [128, D], F32)
    gnorm_b = sb("gnorm_b", [128, DA], F32)
    tabE = sb("tabE", [128, 16, 256], BF16)
    rmask = sb("rmask", [128, 512], F32)
    ident_f = sb("ident_f", [128, 128], F32)
    ident_b = sb("ident_b", [128, 128], BF16)
    mask01 = sb("mask01", [128, 64], F32)
    maskAB = sb("maskAB", [128, 2, 64], F32)
    gpreT = sb("gpreT", [128, 16], F32)
    lbT = sb("lbT", [128, 8], F32)
    lnoml = sb("lnoml", [128, 8], F32)
    small = sb("small", [128, 64], F32)
    epsc = sb("epsc", [128, 1], F32)
    onec = sb("onec", [128, 1], F32)
    junkS = sb("junkS", [128, D], BF16)
    T_junk = T("junkS")
    PREF = {}
    EL = sb("EL", [128, 8, 8], F32)
    stats = sb("stats", [128, 64], F32)
    rbc = sb("rbc", [16, 1], F32)
    Jrev = sb("Jrev", [128, 128], F32)

    ps = st.enter_context(nc.psum_tensor("ps", [128, 8, 512], F32))
    PB = [T("bank%d" % b) for b in range(8)]

    T_kT = [T("kT%d" % i) for i in range(8)]
    T_va = [T("va%d" % i) for i in range(8)]
    T_S32 = [T("S32_%d" % h) for h in range(8)]
    T_Sbf = [T("Sbf_%d" % h) for h in range(8)]
    T_c = T("consts")
    T_EL = [T("EL%d" % h) for h in range(8)]
    T_stats = T("stats")
    T_small = T("small")
    T_rbext = T("rbext")


    out_deps = []

    r_ = NPT - 2
    sizes = [4] * (r_ // 4) + [2] * ((r_ % 4) // 2) + [2]
    if cfg.get("sizes"):
        sizes = list(cfg["sizes"])
    passes = []
    g0 = 0
    for pi_, sz in enumerate(sizes):
        tl_ = [dict(kind="p", gt=g0 + i, nt=128, col=128 * i) for i in range(sz)]
        g0 += sz
        if pi_ == len(sizes) - 1:
            tl_ += [dict(kind="s", s=i, nt=ST, col=128 * sz + ST * i) for i in range(NSAMP)]
        passes.append(tl_)
    assert g0 == NPT

    out_tiles = set(range(NPT - 4, NPT))

    def xsrc(tl):
        if tl["kind"] == "p":
            return xp[tl["gt"] * 128:(tl["gt"] + 1) * 128, :]
        return xs[tl["s"], :, :]

    def ydst(tl):
        if tl["kind"] == "p":
            return yp[tl["gt"] * 128:(tl["gt"] + 1) * 128, :]
        return ys[tl["s"], :, :]

    _pass0_tiles = passes[0]
    cfg["_sizes"] = list(sizes)

    def A(eng, fn, reads=(), writes=(), inc=True):
        return S.op(eng, fn, reads=reads, writes=writes, inc=inc)

    A("pool", lambda e: e.memset(ident_f[:], 0.0), writes=[T_c])
    A("pool", lambda e: e.affine_select(out=ident_f[:], in_=ident_f[:], compare_op=ALU.not_equal, fill=1.0,
                                        base=0, pattern=[[-1, 128]], channel_multiplier=1),
      reads=[T_c], writes=[T_c])
    A("pool", lambda e: e.tensor_copy(ident_b[:], ident_f[:]), reads=[T_c], writes=[T_c])
    A("pool", lambda e: e.memset(mask01[:], 1.0), writes=[T_c])
    A("pool", lambda e: e.affine_select(out=mask01[0:64, :], in_=mask01[0:64, :], compare_op=ALU.is_ge, fill=0.0,
                                        base=0, pattern=[[1, 64]], channel_multiplier=-1),
      reads=[T_c], writes=[T_c])
    S.dma("sp", lambda e: e.dma_start(out=mask01[64:128, :], in_=mask01[0:64, :]), reads=[T_c], writes=[T_c], key="c_mask")
    A("pool", lambda e: e.memset(rmask[:], 1.0), writes=[T_c])
    A("pool", lambda e: e.memset(rmask[:].rearrange("p (c t) -> p c t", t=64)[:, :, 0:1], 0.0),
      reads=[T_c], writes=[T_c])
    A("pool", lambda e: e.memset(epsc[:], EPS), writes=[T_c])
    A("pool", lambda e: e.memset(onec[:], 1.0), writes=[T_c])
    A("pool", lambda e: e.memset(vaug[:], 1.0), writes=T_va)
    A("pool", lambda e: e.memset(S32[:], 0.0), writes=T_S32)
    A("pool", lambda e: e.memset(Sbf[:], 0.0), writes=T_Sbf)

    S.dma("sp", lambda e: e.dma_start(out=gpost_b[:], in_=norm_post.partition_broadcast(128)), writes=[T_c], key="c_gpost")
    S.dma("sp", lambda e: e.dma_start(out=gnorm_b[:], in_=gnorm.partition_broadcast(128)), writes=[T_c], key="c_gnorm")
    T_cst16 = _t0
    S.dma("sp", lambda e: e.dma_start(out=cst16[:, 0:128], in_=norm_pre.rearrange("o (k p) -> (o k) p", p=128)),
          writes=[T_cst16])
    S.dma("sp", lambda e: e.dma_start(out=cst16[:, 128:256], in_=lb_logits.rearrange("r (h p) -> (r h) p", p=128)),
          writes=[T_cst16])
    A("pe", lambda e: e.transpose(out=ps[:, 0, 0:16], in_=cst16[:, 0:128], identity=ident_f[0:16, 0:16]),
      reads=[T_c, T_cst16], writes=[PB[0]])
    A("pe", lambda e: e.transpose(out=ps[:, 0, 16:32], in_=cst16[:, 128:256], identity=ident_f[0:16, 0:16]),
      reads=[T_c, T_cst16], writes=[PB[0]])
    A("dve", lambda e: e.tensor_copy(gpreT[:], ps[:, 0, 0:16]), writes=[PB[0], T_c])
    A("dve", lambda e: e.tensor_copy(small[:, 0:16], ps[:, 0, 16:32]), writes=[PB[0], T_small])
    A("dve", lambda e: e.tensor_tensor(out=small[:, 16:24], in0=small[:, 8:16], in1=small[:, 0:8], op=ALU.subtract),
      reads=[T_small], writes=[T_small])
    A("act", lambda e: e.activation(out=small[:, 24:32], in_=small[:, 16:24], func=AF.Exp),
      reads=[T_small], writes=[T_small])
    A("dve", lambda e: e.tensor_scalar_add(small[:, 32:40], small[:, 24:32], 1.0), reads=[T_small], writes=[T_small])
    A("dve", lambda e: e.reciprocal(lbT[:], small[:, 32:40]), reads=[T_small], writes=[T_c])
    A("dve", lambda e: e.tensor_tensor(out=small[:, 40:48], in0=small[:, 24:32], in1=lbT[:], op=ALU.mult),
      reads=[T_small, T_c], writes=[T_small])
    A("act", lambda e: e.activation(out=lnoml[:], in_=small[:, 40:48], func=AF.Ln), reads=[T_small], writes=[T_c])

    A("pool", lambda e: e.memset(Jrev[:], 0.0), writes=[T_c])
    A("pool", lambda e: e.affine_select(out=Jrev[:], in_=Jrev[:], compare_op=ALU.not_equal, fill=1.0,
                                        base=-127, pattern=[[1, 128]], channel_multiplier=1),
      reads=[T_c], writes=[T_c])

    init_deps = [(e_, S.cnt[e_]) for e_ in S.ENGS if S.cnt[e_] > 0] + [(k_, S.dma_sems[k_]) for k_ in S.key_sems.values()
                                                                        if S.dma_sems[k_] > 0]
    for e_ in ("pe", "act", "dve"):
        S.wait_all(e_, init_deps)
    T_mab = T("maskAB")
    A("dve", lambda e: e.tensor_copy(maskAB[:, 0, :], mask01[:]), reads=[T_c], writes=[T_mab])
    A("dve", lambda e: e.tensor_copy(maskAB[:, 1, :], mask01[:]), reads=[T_c], writes=[T_mab])
    A("dve", lambda e: e.memset(maskAB[64:128, 0, :], 0.0), writes=[T_mab])
    A("dve", lambda e: e.memset(maskAB[0:64, 1, :], 0.0), writes=[T_mab])
    if plan is not None:
        for ap_ in (w_in, w_pa, w_pb, w_out, w_in_bf, w_pa_bf, w_pb_bf, w_out_bf):
            plan.bind(ap_)
        if cfg.get("scratch", True):
            n_passes_ = len(cfg["_sizes"])
            assert len(plan) % n_passes_ == 0
            W.n_req = len(plan) // n_passes_
            W.tscr = [T("scr%d" % j_) for j_ in range(W.n_req)]
    for i0_, tl0_ in enumerate(_pass0_tiles):
        t_x0, x_ap0 = AR.take("x%d" % i0_, LOC + i0_ * 8192, 8192)
        S.dma("sp", lambda e, d=x_ap0[0:tl0_["nt"], :], s=xsrc(tl0_): e.dma_start(out=d, in_=s), writes=[t_x0])
        PREF[i0_] = (t_x0, x_ap0)

    T_rbsb = _t1
    T_tab = T("tabE")
    traw_ap = kT[:].rearrange("p c t -> p (c t)").bitcast(F32)
    traw = traw_ap.rearrange("p (h c) -> p h c", c=256)

    def table_dma():
        S.dma("sp", lambda e: e.dma_start(out=rbsb[:, 0:257], in_=rel_bias), writes=[T_rbsb])
        A("dve", lambda e: e.tensor_copy(rbsb[:, 257:384], rbsb[:, 256:257].to_broadcast([16, 127])), writes=[T_rbsb])
        A("dve", lambda e: e.tensor_copy(rbc[:], rbsb[:, 256:257]), reads=[T_rbsb], writes=[T_small])
        A("dve", lambda e: e.tensor_scalar(out=rbsb[:], in0=rbsb[:], scalar1=rbc[:, 0:1], scalar2=None, op0=ALU.subtract),
          reads=[T_small], writes=[T_rbsb])
        S.dma("sp", lambda e: e.dma_start(out=rb_ext, in_=rbsb[:]), reads=[T_rbsb], writes=[T_rbext])
        for h in range(16):
            for r in range(2):
                base = h * 384 + (129 if r == 0 else 1)
                src = bass.AP(tensor=rb_ext.tensor, offset=rb_ext.offset + base, ap=[[1, 128], [1, 128]])
                dst = traw[:, h, r * 128:(r + 1) * 128]
                S.dma("sp", lambda e, dst=dst, src=src: e.dma_start(out=dst, in_=src), reads=[T_rbext], writes=T_kT,
                      key="tab_raw")

    def table_compute():
        for q in range(8):
            bq = 5 + (q % 2)
            A("pe", lambda e, q=q, bq=bq: e.matmul(ps[:, bq, :], lhsT=Jrev[:], rhs=traw_ap[:, q * 512:(q + 1) * 512], start=True, stop=True),
              reads=[T_c] + T_kT, writes=[PB[bq]])
            A("act", lambda e, q=q, bq=bq: e.activation(out=tabE[:, 2 * q:2 * q + 2, :], in_=ps[:, bq, :].rearrange("p (h c) -> p h c", c=256),
                                                         func=AF.Exp),
              writes=[PB[bq], T_tab])
        A("dve", lambda e: e.memset(tabE[64:128, :, 128:192], 0.0), writes=[T_tab])

    bank_rr = [0]

    def rr_bank(lo, hi):
        b = lo + (bank_rr[0] % (hi - lo))
        bank_rr[0] += 1
        return b

    def do_pass(pi, tiles):
        ntl = len(tiles)
        TT = sum(t["nt"] for t in tiles)
        nchunk = TT // 64

        T_xn, xn_ap = [], None
        t_, xn_raw = AR.take("xnT", 0, 16 * 1024)
        xnT = xn_raw.bitcast(BF16).rearrange("p (k t) -> p k t", t=512)
        T_xn = [t_]
        t_oa, oa_raw = AR.take("oaT", 16 * 1024, 8 * 1024)
        oaT = oa_raw.bitcast(BF16).rearrange("p (k t) -> p k t", t=512)
        t_ob, ob_raw = AR.take("obT", 24 * 1024, 8 * 1024)
        obT = ob_raw.bitcast(BF16).rearrange("p (k t) -> p k t", t=512)

        xt = []
        for i, tl in enumerate(tiles):
            if i in PREF:
                xt.append(PREF.pop(i))
                continue
            t_x, x_ap = AR.take("x%d" % i, LOC + i * 8192, 8192)
            xt.append((t_x, x_ap))
            nt = tl["nt"]
            S.dma("sp", lambda e, d=x_ap[0:nt, :], s=xsrc(tl): e.dma_start(out=d, in_=s), writes=[t_x])
        t_junk, junk = T_junk, junkS
        for i, tl in enumerate(tiles):
            nt, col = tl["nt"], tl["col"]
            t_x, x_ap = xt[i]
            ssq = stats[0:nt, 0:1]
            sd = stats[0:nt, 1:2]
            rs = stats[0:nt, 2:3]
            A("act", lambda e, x=x_ap[0:nt, :], j=junk[0:nt, :], a=ssq: e.activation(out=j, in_=x, func=AF.Square, accum_out=a),
              reads=[t_x], writes=[t_junk, T_stats])
            A("act", lambda e, a=ssq, o=sd, n=nt: e.activation(out=o, in_=a, func=AF.Sqrt, scale=1.0 / D, bias=epsc[0:n, :]),
              reads=[T_c], writes=[T_stats])
            A("dve", lambda e, o=rs, i_=sd: e.reciprocal(o, i_), writes=[T_stats])
            A("dve", lambda e, x=x_ap[0:nt, :], r=rs: e.tensor_scalar(out=x, in0=x, scalar1=r, scalar2=None, op0=ALU.mult),
              reads=[T_stats], writes=[t_x])
            for kb in range(4):
                b = 6 + (kb % 2)
                for kk in range(4):
                    k = kb * 4 + kk
                    A("pe", lambda e, b=b, kk=kk, k=k, x=x_ap, nt=nt: e.transpose(
                        out=ps[:, b, kk * nt:(kk + 1) * nt], in_=x[0:nt, k * 128:(k + 1) * 128], identity=ident_f[0:nt, 0:nt]),
                      reads=[t_x, T_c], writes=[PB[b]], inc=(kk == 3))
                A("dve", lambda e, b=b, kb=kb, nt=nt, col=col: e.tensor_tensor(
                    out=xnT[:, kb * 4:(kb + 1) * 4, col:col + nt],
                    in0=ps[:, b, 0:4 * nt].rearrange("p (a t) -> p a t", t=nt),
                    in1=gpreT[:, kb * 4:(kb + 1) * 4].unsqueeze(2).to_broadcast([128, 4, nt]), op=ALU.mult),
                  reads=[T_c], writes=[PB[b], T_xn[0]])

        off = LOC
        t_qt, qt_raw = AR.take("qt", off, 8192); off += 8192
        qt = qt_raw.bitcast(BF16).rearrange("p (h t) -> p h t", t=512)
        t_kt, kt_raw = AR.take("kt", off, 8192); off += 8192
        kt = kt_raw.bitcast(BF16).rearrange("p (h t) -> p h t", t=512)
        vsb, Gsb = [], []
        for i in range(ntl):
            t1, a1 = AR.take("vsb%d" % i, off, 2048); off += 2048
            vsb.append((t1, a1.bitcast(BF16)))
        for i in range(ntl):
            t1, a1 = AR.take("G%d" % i, off, 2048); off += 2048
            Gsb.append((t1, a1.bitcast(BF16)))
        off = LOC + 32768
        t_B4, B4_raw = AR.take("B4", off, 8192); off += 8192
        B4 = B4_raw.rearrange("p (h t) -> p h t", t=512)
        tmp = {}
        for nm in ("U", "L1", "L2", "Wq", "L3"):
            for par in range(2):
                tmp[nm, par] = AR.take("%s_%d" % (nm, par), off, 2048); off += 2048
        assert off == LOC + 60 * 1024
        t_sqj1, sqj1 = AR.take("sqj1", LOC + 68 * 1024, 2048)
        B4_off = LOC + 32768

        def wsrc_in(c0, kh):
            return w_in_v[:, kh * 8:(kh + 1) * 8, c0:c0 + 512]

        def proj_fm(bank, wl, cl, rhs_all, n):
            nk = 8 * len(wl)
            for k in range(nk):
                wap, wt = wl[k // 8]
                A("pe", lambda e, bank=bank, wap=wap, k=k, cl=cl, n=n, rhs_all=rhs_all, nk=nk: e.matmul(
                    ps[:, bank, 0:n], lhsT=wap[:, k % 8, cl * 128:(cl + 1) * 128], rhs=rhs_all[:, k, 0:n],
                    start=(k == 0), stop=(k == nk - 1)),
                  reads=[wt, rd_T], writes=[PB[bank]], inc=(k == nk - 1))

        def proj_tm(bank, wl, lhs_all, col, nt, ncol=512):
            nk = 8 * len(wl)
            for k in range(nk):
                wap, wt = wl[k // 8]
                A("pe", lambda e, bank=bank, wap=wap, k=k, col=col, nt=nt, lhs_all=lhs_all, nk=nk, ncol=ncol: e.matmul(
                    ps[0:nt, bank, 0:ncol], lhsT=lhs_all[:, k, col:col + nt], rhs=wap[:, k % 8, 0:ncol],
                    start=(k == 0), stop=(k == nk - 1)),
                  reads=[wt, rd_T], writes=[PB[bank]], inc=(k == nk - 1))

        rd_T = T_xn[0]
        for hg in range(2):
            wl = [W.acquire(wsrc_in(1024 + hg * 512, 0)), W.acquire(wsrc_in(1024 + hg * 512, 1))]
            fbank = {}

            def f_part1(hl, hg=hg, wl=wl):
                h = hg * 4 + hl
                bank = rr_bank(0, 4)
                fbank[hl] = bank
                proj_fm(bank, wl, hl, xnT, TT)
                par = hl % 2
                tU, U = tmp["U", par]; tL1, L1 = tmp["L1", par]; tL2, L2 = tmp["L2", par]
                zf = ps[:, bank, 0:TT]
                A("act", lambda e, zf=zf, U=U: e.activation(out=U[:, 0:TT], in_=zf, func=AF.Exp, scale=-1.0),
                  writes=[PB[bank], tU])
                A("act", lambda e, U=U, L1=L1: e.activation(out=L1[:, 0:TT], in_=U[:, 0:TT], func=AF.Ln, bias=onec[:, 0:1]),
                  reads=[tU, T_c], writes=[tL1])
                A("act", lambda e, U=U, L2=L2, h=h: e.activation(out=L2[:, 0:TT], in_=U[:, 0:TT], func=AF.Ln,
                                                                  scale=lbT[:, h:h + 1], bias=onec[:, 0:1]),
                  reads=[tU, T_c], writes=[tL2])
                A("dve", lambda e, L1=L1, L2=L2: e.tensor_tensor(out=L2[:, 0:TT], in0=L2[:, 0:TT], in1=L1[:, 0:TT], op=ALU.subtract),
                  reads=[tL1], writes=[tL2])
                A("dve", lambda e, L2=L2, hl=hl: e.tensor_tensor_scan(out=B4[:, hl, 0:TT], data0=rmask[:, 0:TT], data1=L2[:, 0:TT],
                                                                       initial=0.0, op0=ALU.mult, op1=ALU.add),
                  reads=[tL2, T_c], writes=[t_B4])
                A("dve", lambda e, L1=L1, hl=hl: e.tensor_tensor(out=L1[:, 0:TT], in0=L1[:, 0:TT], in1=B4[:, hl, 0:TT], op=ALU.add),
                  reads=[t_B4], writes=[tL1])
                A("dve", lambda e, zf=zf, L1=L1, U=U: e.tensor_tensor(out=U[:, 0:TT], in0=zf, in1=L1[:, 0:TT], op=ALU.add),
                  reads=[tL1], writes=[PB[bank], tU])

            def f_part2(hl, hg=hg):
                h = hg * 4 + hl
                tU, U = tmp["U", hl % 2]
                A("act", lambda e, U=U, h=h: e.activation(out=kt[:, h, 0:TT], in_=U[:, 0:TT], func=AF.Exp, scale=-1.0,
                                                           bias=lnoml[:, h:h + 1]),
                  reads=[tU, T_c], writes=[t_kt])
                A("act", lambda e, hl=hl, h=h: e.activation(
                    out=EL[:, h, 0:nchunk], in_=B4[:, hl, 0:TT].rearrange("p (c t) -> p c t", t=64)[:, :, 63], func=AF.Exp),
                  reads=[t_B4], writes=[T_EL[h]])

            for hl in range(4):
                f_part1(hl)
                if hl >= 1:
                    f_part2(hl - 1)
            f_part2(3)
            W.release(2)
            wl = [W.acquire(wsrc_in(hg * 512, 0)), W.acquire(wsrc_in(hg * 512, 1))]
            qbank = {}

            def q_part1(hl, hg=hg, wl=wl):
                bank = rr_bank(0, 4)
                qbank[hl] = bank
                proj_fm(bank, wl, hl, xnT, TT)
                par = hl % 2
                tW, Wq = tmp["Wq", par]; tL3, L3 = tmp["L3", par]
                zq = ps[:, bank, 0:TT]
                A("act", lambda e, zq=zq, Wq=Wq: e.activation(out=Wq[:, 0:TT], in_=zq, func=AF.Exp, scale=-1.0),
                  writes=[PB[bank], tW])
                A("act", lambda e, Wq=Wq, L3=L3: e.activation(out=L3[:, 0:TT], in_=Wq[:, 0:TT], func=AF.Ln, bias=onec[:, 0:1]),
                  reads=[tW, T_c], writes=[tL3])
                A("dve", lambda e, L3=L3, hl=hl: e.tensor_tensor(out=L3[:, 0:TT], in0=B4[:, hl, 0:TT], in1=L3[:, 0:TT], op=ALU.subtract),
                  reads=[t_B4], writes=[tL3])

            def q_part2(hl, hg=hg):
                h = hg * 4 + hl
                par = hl % 2
                bank = qbank[hl]
                tW, Wq = tmp["Wq", par]; tL3, L3 = tmp["L3", par]
                zq = ps[:, bank, 0:TT]
                A("act", lambda e, Wq=Wq, L3=L3: e.activation(out=Wq[:, 0:TT], in_=L3[:, 0:TT], func=AF.Exp),
                  reads=[tL3], writes=[tW])
                A("dve", lambda e, zq=zq, Wq=Wq, h=h: e.tensor_tensor(out=qt[:, h, 0:TT], in0=zq, in1=Wq[:, 0:TT], op=ALU.mult),
                  reads=[tW], writes=[PB[bank], t_qt])

            for hl in range(4):
                q_part1(hl)
                if hl >= 1:
                    q_part2(hl - 1)
            q_part2(3)
            W.release(2)
            wl = [W.acquire(wsrc_in(2048 + hg * 512, 0)), W.acquire(wsrc_in(2048 + hg * 512, 1))]
            for i, tl in enumerate(tiles):
                nt, col = tl["nt"], tl["col"]
                bank = rr_bank(4, 8)
                proj_tm(bank, wl, xnT, col, nt)
                tv, vap = vsb[i]
                A("act", lambda e, bank=bank, nt=nt, vap=vap, hg=hg: e.activation(
                    out=vap[0:nt, hg * 512:(hg + 1) * 512], in_=ps[0:nt, bank, :], func=AF.Copy),
                  writes=[PB[bank], tv])
            W.release(2)
            wl = [W.acquire(wsrc_in(3072 + hg * 512, 0)), W.acquire(wsrc_in(3072 + hg * 512, 1))]
            for i, tl in enumerate(tiles):
                nt, col = tl["nt"], tl["col"]
                bank = rr_bank(4, 8)
                proj_tm(bank, wl, xnT, col, nt)
                tg, gap = Gsb[i]
                tq_, sq_ = t_sqj1, sqj1
                A("act", lambda e, bank=bank, nt=nt, sq_=sq_: e.activation(out=sq_[0:nt, 0:512], in_=ps[0:nt, bank, :], func=AF.Silu),
                  writes=[PB[bank], tq_])
                A("dve", lambda e, nt=nt, sq_=sq_, gap=gap, hg=hg: e.tensor_tensor(
                    out=gap[0:nt, hg * 512:(hg + 1) * 512], in0=sq_[0:nt, 0:512], in1=gnorm_b[0:nt, hg * 512:(hg + 1) * 512], op=ALU.mult),
                  reads=[tq_, T_c], writes=[tg])
            W.release(2)

        t_qT, qT_raw = AR.take("qT", LOC + 40 * 1024, 8192)
        qT = qT_raw.bitcast(BF16).rearrange("p (c t) -> p c t", t=512)
        Gb = []
        for i in range(ntl):
            t1, a1 = AR.take("Gb%d" % i, LOC + 60 * 1024 + i * 2048, 2048)
            Gb.append((t1, a1.bitcast(BF16)))
        t_stg, stg = AR.take("stg", LOC + 68 * 1024, 2048)
        t_stg2, stg2 = AR.take("stg2", LOC + 70 * 1024, 2048)
        def kv_rows(tl):
            if tl["kind"] == "s":
                return ks[tl["s"]], vs[tl["s"]]
            if tl["gt"] in out_tiles:
                r0 = (tl["gt"] - (NPT - 4)) * 128
                return kp[r0:r0 + 128, :], vp[r0:r0 + 128, :]
            return None

        def slot_of(tl):
            return (tl["gt"] % 8) if tl["kind"] == "p" else tl["s"]

        ksegs = []
        for tl in tiles:
            if tl["kind"] == "p":
                kc = (tl["gt"] % 8) * 128
                hd = T_kT[tl["gt"] % 8]
            else:
                kc = tl["s"] * ST
                hd = T_kT[0]
            if ksegs and ksegs[-1][0] + ksegs[-1][1] == tl["col"] and ksegs[-1][2] + ksegs[-1][1] == kc:
                p0, n0_, k0_, hs_ = ksegs[-1]
                ksegs[-1] = (p0, n0_ + tl["nt"], k0_, hs_ + ([hd] if hd not in hs_ else []))
            else:
                ksegs.append((tl["col"], tl["nt"], kc, [hd]))

        def gen_A2():
            t_osb, osb = AR.take("osb", LOC + 50 * 1024, 4096)
            t_sqj, sqj = AR.take("sqj", LOC + 54 * 1024, 4096)
            t_og, og_raw = AR.take("og", LOC + 58 * 1024, 2048)
            og = og_raw.bitcast(BF16)
            t_ktok, ktok_raw = AR.take("ktok", B4_off, 2048)
            ktok = ktok_raw.bitcast(BF16).rearrange("p (h d) -> p h d", d=128)
            t_Asb, Asb_raw = AR.take("Asb", B4_off + 2048, 1024)
            Asb = Asb_raw.bitcast(BF16).rearrange("p (h t) -> p h t", t=64)
            t_Asb1, Asb1_raw = AR.take("Asb1", B4_off + 7168, 1024)
            Asb1 = Asb1_raw.bitcast(BF16).rearrange("p (h t) -> p h t", t=64)
            t_s0, _r0 = AR.take("stmp_g0", B4_off + 3072, 2048)
            t_s1, _r1 = AR.take("stmp_g1", B4_off + 5120, 2048)
            t_stm = [t_s0, t_s1]
            stmp = AR.ap[:, (B4_off + 3072) // 4:(B4_off + 7168) // 4].rearrange("p (h d) -> p h d", d=128)
            cidx = 0
            for i, tl in enumerate(tiles):
                nt, col = tl["nt"], tl["col"]
                nch = nt // 64
                tv, vap = vsb[i]
                tg, gap = Gsb[i]
                is_samp = tl["kind"] == "s"
                if is_samp:
                    s = tl["s"]
                    S.dma("sp", lambda e, s=s: e.dma_start(out=S32[:], in_=st_in[s].rearrange("h k v -> k h v")),
                          writes=T_S32)
                    A("act", lambda e: e.activation(out=Sbf[:], in_=S32[:], func=AF.Copy), reads=T_S32, writes=T_Sbf)
                for h in range(8):
                    A("pe", lambda e, h=h, col=col, nt=nt: e.transpose(
                        out=ps[0:nt, 0, :].bitcast(BF16)[:, h * 128:(h + 1) * 128], in_=kt[:, h, col:col + nt], identity=ident_b[:]),
                      reads=[t_kt, T_c], writes=[PB[0]], inc=(h == 7))
                A("act", lambda e, nt=nt: e.activation(out=ktok[0:nt, :, :],
                                                       in_=ps[0:nt, 0, :].bitcast(BF16).rearrange("p (h d) -> p h d", d=128), func=AF.Copy),
                  writes=[PB[0], t_ktok])
                for h in range(8):
                    for cc in range(nch):
                        c0 = col + cc * 64
                        A("pe", lambda e, h=h, c0=c0, cc=cc: e.matmul(
                            ps[cc * 64:(cc + 1) * 64, 0, h * 64:(h + 1) * 64], lhsT=kt[:, h, c0:c0 + 64], rhs=qt[:, h, c0:c0 + 64],
                            start=True, stop=True),
                          reads=[t_kt, t_qt], writes=[PB[0]], inc=(h == 7 and cc == nch - 1))
                if nt == 128:
                    A("dve", lambda e: e.tensor_tensor(
                        out=Asb[:, :, :], in0=ps[:, 0, :].rearrange("p (h t) -> p h t", t=64),
                        in1=maskAB[:, 0, :].unsqueeze(1).to_broadcast([128, 8, 64]), op=ALU.mult),
                      reads=[T_mab], writes=[PB[0], t_Asb])
                    A("dve", lambda e: e.tensor_tensor(
                        out=Asb1[:, :, :], in0=ps[:, 0, :].rearrange("p (h t) -> p h t", t=64),
                        in1=maskAB[:, 1, :].unsqueeze(1).to_broadcast([128, 8, 64]), op=ALU.mult),
                      reads=[T_mab], writes=[PB[0], t_Asb1])
                else:
                    A("dve", lambda e, nt=nt: e.tensor_tensor(
                        out=Asb[0:nt, :, :], in0=ps[0:nt, 0, :].rearrange("p (h t) -> p h t", t=64),
                        in1=mask01[0:nt, :].unsqueeze(1).to_broadcast([nt, 8, 64]), op=ALU.mult),
                      reads=[T_c], writes=[PB[0], t_Asb])
                yield
                for cc in range(nch):
                    c0 = col + cc * 64
                    rows = slice(cc * 64, (cc + 1) * 64)
                    ch = cidx + cc
                    for h in range(8):
                        bo, oc = 1 + h // 4, (h % 4) * 128
                        if nt == 128:
                            Ax, tAx = (Asb, t_Asb) if cc == 0 else (Asb1, t_Asb1)
                            A("pe", lambda e, rows=rows, bo=bo, oc=oc, h=h, vap=vap, Ax=Ax: e.matmul(
                                ps[rows, bo, oc:oc + 128], lhsT=Ax[:, h, :], rhs=vap[:, h * 128:(h + 1) * 128], start=True, stop=False),
                              reads=[tAx, tv], writes=[PB[bo]], inc=False)
                        else:
                            A("pe", lambda e, rows=rows, bo=bo, oc=oc, h=h, vap=vap: e.matmul(
                                ps[rows, bo, oc:oc + 128], lhsT=Asb[rows, h, :], rhs=vap[rows, h * 128:(h + 1) * 128], start=True, stop=False),
                              reads=[t_Asb, tv], writes=[PB[bo]], inc=False)
                        A("pe", lambda e, rows=rows, bo=bo, oc=oc, h=h, c0=c0: e.matmul(
                            ps[rows, bo, oc:oc + 128], lhsT=qt[:, h, c0:c0 + 64], rhs=Sbf[:, h, :], start=False, stop=True),
                          reads=[t_qt, T_Sbf[h]], writes=[PB[bo]], inc=(h % 4 == 3))
                    for h in range(8):
                        bp, pc = 3 + h // 4, (h % 4) * 128
                        A("pe", lambda e, rows=rows, bp=bp, pc=pc, h=h, vap=vap: e.matmul(
                            ps[:, bp, pc:pc + 128], lhsT=ktok[rows, h, :], rhs=vap[rows, h * 128:(h + 1) * 128], start=True, stop=True),
                          reads=[t_ktok, tv], writes=[PB[bp]], inc=(h % 4 == 3))
                    for g in range(2):
                        bp = 3 + g
                        hs = slice(4 * g, 4 * g + 4)
                        A("dve", lambda e, bp=bp, hs=hs: e.tensor_tensor(
                            out=stmp[:, hs, :], in0=ps[:, bp, :].rearrange("p (h d) -> p h d", d=128), in1=S32[:, hs, :], op=ALU.add),
                          reads=T_S32[4 * g:4 * g + 4], writes=[PB[bp], t_stm[g]])
                        A("dve", lambda e, hs=hs, ch=ch: e.tensor_tensor(
                            out=S32[:, hs, :], in0=stmp[:, hs, :], in1=EL[:, hs, ch:ch + 1].to_broadcast([128, 4, 128]), op=ALU.mult),
                          reads=[t_stm[g]] + T_EL[4 * g:4 * g + 4], writes=T_S32[4 * g:4 * g + 4])
                        A("act", lambda e, hs=hs: e.activation(out=Sbf[:, hs, :], in_=S32[:, hs, :], func=AF.Copy),
                          reads=T_S32[4 * g:4 * g + 4], writes=T_Sbf[4 * g:4 * g + 4])
                    yield
                for g in range(2):
                    A("act", lambda e, g=g, nt=nt: e.activation(out=osb[0:nt, g * 512:(g + 1) * 512], in_=ps[0:nt, 1 + g, :], func=AF.Copy),
                      writes=[PB[1 + g], t_osb])
                cidx += nch
                if is_samp:
                    d = S.dma("sp", lambda e, s=tl["s"]: e.dma_start(out=ss_o[s].rearrange("h k v -> k h v"), in_=S32[:]),
                              reads=T_S32)
                elif tl["gt"] == NPT - 1:
                    d = S.dma("sp", lambda e: e.dma_start(out=sp_o.rearrange("h k v -> k h v"), in_=S32[:]),
                              reads=T_S32)
                A("act", lambda e, nt=nt: e.activation(out=sqj[0:nt, :], in_=osb[0:nt, :], func=AF.Square),
                  reads=[t_osb], writes=[t_sqj])
                A("dve", lambda e, nt=nt: e.tensor_reduce(out=stats[0:nt, 8:16], in_=sqj[0:nt, :].rearrange("p (h d) -> p h d", d=128),
                                                          op=ALU.add, axis=AX.X),
                  reads=[t_sqj], writes=[T_stats])
                A("act", lambda e, nt=nt: e.activation(out=stats[0:nt, 16:24], in_=stats[0:nt, 8:16], func=AF.Sqrt, scale=1.0 / 128,
                                                       bias=epsc[0:nt, :]),
                  reads=[T_c], writes=[T_stats])
                A("dve", lambda e, nt=nt: e.reciprocal(stats[0:nt, 24:32], stats[0:nt, 16:24]), writes=[T_stats])
                A("dve", lambda e, nt=nt: e.tensor_tensor(
                    out=osb[0:nt, :].rearrange("p (h d) -> p h d", d=128), in0=osb[0:nt, :].rearrange("p (h d) -> p h d", d=128),
                    in1=stats[0:nt, 24:32].unsqueeze(2).to_broadcast([nt, 8, 128]), op=ALU.mult),
                  reads=[T_stats], writes=[t_osb])
                A("dve", lambda e, nt=nt, gap=gap: e.tensor_tensor(out=og[0:nt, :], in0=osb[0:nt, :], in1=gap[0:nt, :], op=ALU.mult),
                  reads=[t_osb, tg], writes=[t_og])
                bt = 0
                for c in range(8):
                    A("pe", lambda e, c=c, bt=bt, nt=nt: e.transpose(
                        out=ps[:, bt, 0:512].bitcast(BF16)[:, c * 128:c * 128 + nt], in_=og[0:nt, c * 128:(c + 1) * 128],
                        identity=ident_b[0:nt, 0:nt]),
                      reads=[t_og, T_c], writes=[PB[bt]], inc=(c == 7))
                A("act", lambda e, bt=bt, nt=nt, col=col: e.activation(
                    out=oaT[:, :, col:col + nt], in_=ps[:, bt, 0:512].bitcast(BF16).rearrange("p (c t) -> p c t", t=128)[:, :, 0:nt],
                    func=AF.Copy),
                  writes=[PB[bt], t_oa])
                yield


        def gen_B1():
            for hg in range(2):
                wl = [W.acquire(wsrc_in(4096 + hg * 512, 0)), W.acquire(wsrc_in(4096 + hg * 512, 1))]
                for c in range(4):
                    cc = hg * 4 + c
                    bank = rr_bank(5, 8)
                    proj_fm(bank, wl, c, xnT, TT)
                    A("act", lambda e, bank=bank, cc=cc: e.activation(out=qT[:, cc, 0:TT], in_=ps[:, bank, 0:TT], func=AF.Copy, scale=0.125),
                      writes=[PB[bank], t_qT])
                    yield
                W.release(2)
                wl = [W.acquire(wsrc_in(5120 + hg * 512, 0)), W.acquire(wsrc_in(5120 + hg * 512, 1))]
                for c in range(4):
                    cc = hg * 4 + c
                    bank = rr_bank(5, 8)
                    proj_fm(bank, wl, c, xnT, TT)
                    for (p0, n_, k0_, hs_) in ksegs:
                        A("dve", lambda e, bank=bank, cc=cc, p0=p0, n_=n_, k0_=k0_: e.tensor_copy(
                            kT[:, cc, k0_:k0_ + n_], ps[:, bank, p0:p0 + n_]),
                          writes=[PB[bank]] + hs_)
                    yield
                for i, tl in enumerate(tiles):
                    dst = kv_rows(tl)
                    if dst is None:
                        continue
                    nt, col = tl["nt"], tl["col"]
                    bank = rr_bank(5, 8)
                    proj_tm(bank, wl, xnT, col, nt)
                    A("act", lambda e, bank=bank, nt=nt: e.activation(out=stg[0:nt, :], in_=ps[0:nt, bank, :], func=AF.Copy),
                      writes=[PB[bank], t_stg])
                    d = S.dma("sp", lambda e, nt=nt, dd=dst[0][:, hg * 512:(hg + 1) * 512]: e.dma_start(out=dd, in_=stg[0:nt, :]),
                              reads=[t_stg])
                    yield
                W.release(2)
                wl = [W.acquire(wsrc_in(6144 + hg * 512, 0)), W.acquire(wsrc_in(6144 + hg * 512, 1))]
                for i, tl in enumerate(tiles):
                    nt, col = tl["nt"], tl["col"]
                    bank = rr_bank(5, 8)
                    proj_tm(bank, wl, xnT, col, nt)
                    sl = slot_of(tl)
                    A("dve", lambda e, bank=bank, nt=nt, sl=sl, hg=hg: e.tensor_copy(
                        vaug[0:nt, sl, hg * 8:(hg + 1) * 8, 0:64], ps[0:nt, bank, :].rearrange("p (h d) -> p h d", d=64)),
                      writes=[PB[bank], T_va[sl]])
                    dst = kv_rows(tl)
                    if dst is not None:
                        A("act", lambda e, bank=bank, nt=nt: e.activation(out=stg2[0:nt, :], in_=ps[0:nt, bank, :], func=AF.Copy),
                          writes=[PB[bank], t_stg2])
                        d = S.dma("sp", lambda e, nt=nt, dd=dst[1][:, hg * 512:(hg + 1) * 512]: e.dma_start(out=dd, in_=stg2[0:nt, :]),
                                  reads=[t_stg2])
                    yield
                W.release(2)
                wl = [W.acquire(wsrc_in(7168 + hg * 512, 0)), W.acquire(wsrc_in(7168 + hg * 512, 1))]
                for i, tl in enumerate(tiles):
                    nt, col = tl["nt"], tl["col"]
                    bank = rr_bank(5, 8)
                    proj_tm(bank, wl, xnT, col, nt)
                    tg, gap = Gb[i]
                    A("act", lambda e, bank=bank, nt=nt, gap=gap, hg=hg: e.activation(
                        out=gap[0:nt, hg * 512:(hg + 1) * 512], in_=ps[0:nt, bank, :], func=AF.Silu),
                      writes=[PB[bank], tg])
                    yield
                W.release(2)


        if pi == 0:
            table_compute()
        gA, gB = gen_A2(), gen_B1()
        doneA = doneB = False
        while not (doneA and doneB):
            if not doneA:
                try:
                    next(gA)
                except StopIteration:
                    doneA = True
            for _ in range(2):
                if not doneB:
                    try:
                        next(gB)
                    except StopIteration:
                        doneB = True

        off = LOC
        Pex = []
        for j in range(2):
            t1, a1 = AR.take("Pex%d" % j, off, 2560); off += 2560
            Pex.append((t1, a1))
        PTt = []
        for j in range(2):
            t1, a1 = AR.take("PT%d" % j, off, 1280); off += 1280
            PTt.append((t1, a1.bitcast(BF16)))
        t_on, on_ap = AR.take("on", off, 4096); off += 4096
        t_ogb, ogb_raw = AR.take("ogb", off, 2048); off += 2048
        ogb = ogb_raw.bitcast(BF16)
        t_kc, kc_raw = AR.take("kctok", off, 8192); off += 8192
        kctok = kc_raw.bitcast(BF16).rearrange("p (r c) -> p r c", c=1024)
        assert off <= LOC + 32 * 1024

        for i, tl in enumerate(tiles):
            nt, col = tl["nt"], tl["col"]
            tg, gap = Gb[i]
            is_samp = tl["kind"] == "s"
            if is_samp:
                s = tl["s"]
                S.dma("pool", lambda e, s=s: e.dma_start(out=kctok, in_=ck[s].rearrange("(r p) c -> p r c", p=128)),
                      writes=[t_kc])
                for r in range(4):
                    S.dma("pool", lambda e, s=s, r=r: e.dma_start(
                        out=vaug[:, 2 + r, :, 0:64], in_=cv[s, r * 128:(r + 1) * 128, :].rearrange("p (h d) -> p h d", d=64)),
                          writes=[T_va[2 + r]])
                for r in range(4):
                    bt = 6 + (r % 2)
                    for cc in range(8):
                        A("pe", lambda e, r=r, cc=cc, bt=bt: e.transpose(
                            out=ps[:, bt, 0:512].bitcast(BF16)[:, cc * 128:(cc + 1) * 128], in_=kctok[:, r, cc * 128:(cc + 1) * 128],
                            identity=ident_b[:]),
                          reads=[t_kc, T_c], writes=[PB[bt]], inc=(cc == 7))
                    A("dve", lambda e, r=r, bt=bt: e.tensor_copy(
                        kT[:, :, (1 + r) * 128:(2 + r) * 128], ps[:, bt, 0:512].bitcast(BF16).rearrange("p (c t) -> p c t", t=128)),
                      writes=[PB[bt], T_kT[1 + r]])
                blocks = [((1 + r) * 128, 128, 2 + r, (0 if r == 3 else None), "full", 1 + r) for r in range(4)]
                blocks.append((s * ST, 64, s, 1, "own", 0))
            else:
                gt = tl["gt"]
                blocks = []
                for r in range(5):
                    j = gt - 4 + r
                    if j < 0:
                        continue
                    sl = j % 8
                    kind = "first" if r == 0 else ("own" if r == 4 else "full")
                    blocks.append((sl * 128, 128, sl, {3: 0, 4: 1}.get(r), kind, sl))
            nb = len(blocks)

            def emit_ST(h, blocks=blocks, nb=nb, nt=nt, col=col):
                c, pb = h // 2, (h % 2) * 64
                b0 = 2 * (h % 2)
                for bi, (kc0, nk, sl, tb, kind, kh) in enumerate(blocks):
                    bank = b0 if bi < 4 else b0 + 1
                    cb = (bi % 4) * 128
                    krd = [T_kT[kh]]
                    A("pe", lambda e, bank=bank, cb=cb, nk=nk, kc0=kc0, pb=pb, c=c, col=col, nt=nt: e.matmul(
                        ps[0:nk, bank, cb:cb + nt], lhsT=kT[pb:pb + 64, c, kc0:kc0 + nk], rhs=qT[pb:pb + 64, c, col:col + nt],
                        start=True, stop=True),
                      reads=krd + [t_qT], writes=[PB[bank]], inc=(bi == nb - 1 or bi == 3))

            def emit_soft(h, blocks=blocks, nb=nb, nt=nt):
                b0 = 2 * (h % 2)
                tP, Pe = Pex[h % 2]
                tPT, PT = PTt[h % 2]
                ntab = sum(1 for bl in blocks if bl[3] is not None)
                nplain = nb - ntab
                if nplain > 0:
                    A("act", lambda e, b0=b0, nplain=nplain, PT=PT, nt=nt: e.activation(
                        out=PT[:, 0:nplain * 128].rearrange("p (b t) -> p b t", t=128)[:, :, 0:nt],
                        in_=ps[:, b0, 0:nplain * 128].rearrange("p (b t) -> p b t", t=128)[:, :, 0:nt], func=AF.Exp),
                      writes=[PB[b0], tPT])
                if nt == 128 and blocks[0][4] == "first":
                    A("dve", lambda e, PT=PT: e.memset(PT[0:64, 64:128], 0.0), writes=[tPT])
                for bi, (kc0, nk, sl, tb, kind, kh) in enumerate(blocks):
                    if tb is None:
                        continue
                    bank = b0 if bi < 4 else b0 + 1
                    cbp = (bi % 4) * 128
                    cb = bi * 128
                    A("act", lambda e, bank=bank, cbp=cbp, cb=cb, nk=nk, nt=nt, Pe=Pe: e.activation(
                        out=Pe[0:nk, cb:cb + nt], in_=ps[0:nk, bank, cbp:cbp + nt], func=AF.Exp),
                      writes=[PB[bank], tP])
                tabs = [(bi, bl) for bi, bl in enumerate(blocks) if bl[3] is not None]
                if len(tabs) == 2 and nt == 128 and tabs[0][1][1] == 128 and tabs[1][1][1] == 128 and tabs[0][1][3] == 0:
                    cb = tabs[0][0] * 128
                    A("dve", lambda e, PT=PT, Pe=Pe, cb=cb, h=h: e.tensor_tensor(
                        out=PT[:, cb:cb + 256], in0=Pe[:, cb:cb + 256], in1=tabE[:, h, 0:256], op=ALU.mult),
                      reads=[tP, T_tab], writes=[tPT])
                else:
                    for bi, (kc0, nk, sl, tb, kind, kh) in tabs:
                        cb = bi * 128
                        A("dve", lambda e, PT=PT, Pe=Pe, cb=cb, nt=nt, nk=nk, tb=tb, h=h: e.tensor_tensor(
                            out=PT[0:nk, cb:cb + nt], in0=Pe[0:nk, cb:cb + nt], in1=tabE[0:nk, h, tb * 128:tb * 128 + nt], op=ALU.mult),
                          reads=[tP, T_tab], writes=[tPT])

            def emit_PV(h, blocks=blocks, nb=nb, nt=nt, gap=gap, tg=tg):
                tPT, PT = PTt[h % 2]
                bpv = 4 + (h // 7)
                hh = h % 7
                O = lambda lo, hi, bpv=bpv, hh=hh: ps[lo:hi, bpv, hh * 65:hh * 65 + 65]
                mm = []
                for bi, (kc0, nk, sl, tb, kind, kh) in enumerate(blocks):
                    cb = bi * 128
                    if kind == "full" or (nt == 128 and nk == 128):
                        mm.insert(0, (0, nt, PT[0:128, cb:cb + nt], vaug[0:128, sl, h, :], [T_va[sl]]))
                for bi, (kc0, nk, sl, tb, kind, kh) in enumerate(blocks):
                    cb = bi * 128
                    if nt == 128 and nk == 128:
                        continue
                    if kind == "first":
                        mm.append((0, 64, PT[0:128, cb:cb + 64], vaug[0:128, sl, h, :], [T_va[sl]]))
                        mm.append((64, 128, PT[64:128, cb + 64:cb + 128], vaug[64:128, sl, h, :], [T_va[sl]]))
                    elif kind == "own":
                        if nt == 128:
                            mm.append((0, 64, PT[0:64, cb:cb + 64], vaug[0:64, sl, h, :], [T_va[sl]]))
                            mm.append((64, 128, PT[0:128, cb + 64:cb + 128], vaug[0:128, sl, h, :], [T_va[sl]]))
                        else:
                            mm.append((0, 64, PT[0:64, cb:cb + 64], vaug[0:64, sl, h, :], [T_va[sl]]))
                covered = [False, False]
                hv = [([0, 1] if (lo == 0 and hi == 128) else ([0] if lo == 0 else [1])) for (lo, hi, _, _, _) in mm]
                last_of = {}
                for mi, halves in enumerate(hv):
                    for x in halves:
                        last_of[x] = mi
                for mi, (lo, hi, l_ap, r_ap, rds) in enumerate(mm):
                    halves = hv[mi]
                    st_ = not all(covered[x] for x in halves)
                    for x in halves:
                        covered[x] = True
                    last = (mi == len(mm) - 1)
                    sp_ = any(last_of[x] == mi for x in halves)
                    A("pe", lambda e, lo=lo, hi=hi, l_ap=l_ap, r_ap=r_ap, st_=st_, sp_=sp_, O=O: e.matmul(
                        O(lo, hi), lhsT=l_ap, rhs=r_ap, start=st_, stop=sp_),
                      reads=[tPT] + rds, writes=[PB[bpv]], inc=last)
                if not (hh == 6 or h == 15):
                    return None

                def norm(h=h, hh=hh, bpv=bpv, nt=nt, gap=gap, tg=tg):
                    nh = hh + 1
                    h0 = h - hh
                    pv = ps[0:nt, bpv, 0:nh * 65].rearrange("p (h d) -> p h d", d=65)
                    A("dve", lambda e, pv=pv, nt=nt, nh=nh: e.reciprocal(stats[0:nt, 32:32 + nh], pv[:, :, 64]),
                      writes=[PB[bpv], T_stats])
                    onv = on_ap[0:nt, h0 * 64:(h0 + nh) * 64].rearrange("p (h d) -> p h d", d=64)
                    A("dve", lambda e, pv=pv, nt=nt, nh=nh, onv=onv: e.tensor_tensor(
                        out=onv, in0=pv[:, :, 0:64], in1=stats[0:nt, 32:32 + nh].unsqueeze(2).to_broadcast([nt, nh, 64]), op=ALU.mult),
                      reads=[T_stats], writes=[PB[bpv], t_on])
                    A("dve", lambda e, nt=nt, nh=nh, h0=h0, gap=gap: e.tensor_tensor(
                        out=ogb[0:nt, h0 * 64:(h0 + nh) * 64], in0=on_ap[0:nt, h0 * 64:(h0 + nh) * 64],
                        in1=gap[0:nt, h0 * 64:(h0 + nh) * 64], op=ALU.mult),
                      reads=[t_on, tg], writes=[t_ogb])
                return norm

            emit_ST(0)
            pending = None
            for h in range(16):
                if h + 1 < 16:
                    emit_ST(h + 1)
                emit_soft(h)
                if pending is not None:
                    pending()
                    pending = None
                pending = emit_PV(h)
            if pending is not None:
                pending()
            bt = 7
            for c in range(8):
                A("pe", lambda e, c=c, bt=bt, nt=nt: e.transpose(
                    out=ps[:, bt, 0:512].bitcast(BF16)[:, c * 128:c * 128 + nt], in_=ogb[0:nt, c * 128:(c + 1) * 128],
                    identity=ident_b[0:nt, 0:nt]),
                  reads=[t_ogb, T_c], writes=[PB[bt]], inc=(c == 7))
            A("act", lambda e, bt=bt, nt=nt, col=col: e.activation(
                out=obT[:, :, col:col + nt], in_=ps[:, bt, 0:512].bitcast(BF16).rearrange("p (c t) -> p c t", t=128)[:, :, 0:nt],
                func=AF.Copy),
              writes=[PB[bt], t_ob])

        off = LOC
        t_sa, sa_raw = AR.take("sa", off, 8192); off += 8192
        sa = sa_raw.rearrange("p (c t) -> p c t", t=512)
        t_sb_, sb_raw = AR.take("sbg", off, 8192); off += 8192
        sbg = sb_raw.rearrange("p (c t) -> p c t", t=512)
        t_mT, mT_raw = AR.take("mT", off, 16384); off += 16384
        mT = mT_raw.bitcast(BF16).rearrange("p (k t) -> p k t", t=512)
        xr = []
        for i in range(ntl):
            t1, a1 = AR.take("xr%d" % i, off, 8192); off += 8192
            xr.append((t1, a1))
        assert off <= ARENA_BYTES
        for i, tl in enumerate(tiles):
            nt = tl["nt"]
            t1, a1 = xr[i]
            S.dma("sp", lambda e, d=a1[0:nt, :], s=xsrc(tl): e.dma_start(out=d, in_=s), writes=[t1])

        nxt_tiles = passes[pi + 1] if pi + 1 < len(passes) else None

        def prefetch_x(ids):
            for i2 in ids:
                if nxt_tiles is None or i2 >= len(nxt_tiles):
                    continue
                tl2 = nxt_tiles[i2]
                t_x2, x_ap2 = AR.take("x%d" % i2, LOC + i2 * 8192, 8192)
                S.dma("sp", lambda e, d=x_ap2[0:tl2["nt"], :], s=xsrc(tl2): e.dma_start(out=d, in_=s), writes=[t_x2])
                PREF[i2] = (t_x2, x_ap2)

        for cg in range(4):
            wl = [W.acquire(wsrc_in(8192 + cg * 512, 0)), W.acquire(wsrc_in(8192 + cg * 512, 1))]
            for c in range(4):
                bank = rr_bank(0, 4)
                proj_fm(bank, wl, c, xnT, TT)
                A("act", lambda e, bank=bank, c=c: e.activation(out=sa[:, c, 0:TT], in_=ps[:, bank, 0:TT], func=AF.Sigmoid),
                  writes=[PB[bank], t_sa])
            W.release(2)
            wl = [W.acquire(w_pa_v[:, :, cg * 512:(cg + 1) * 512])]
            rd_T = t_oa
            for c in range(4):
                bank = rr_bank(4, 8)
                proj_fm(bank, wl, c, oaT, TT)
                A("dve", lambda e, bank=bank, c=c: e.tensor_tensor(out=sa[:, c, 0:TT], in0=ps[:, bank, 0:TT], in1=sa[:, c, 0:TT], op=ALU.mult),
                  writes=[PB[bank], t_sa])
            W.release(1)
            rd_T = T_xn[0]
            wl = [W.acquire(wsrc_in(10240 + cg * 512, 0)), W.acquire(wsrc_in(10240 + cg * 512, 1))]
            for c in range(4):
                bank = rr_bank(0, 4)
                proj_fm(bank, wl, c, xnT, TT)
                A("act", lambda e, bank=bank, c=c: e.activation(out=sbg[:, c, 0:TT], in_=ps[:, bank, 0:TT], func=AF.Sigmoid),
                  writes=[PB[bank], t_sb_])
            W.release(2)
            wl = [W.acquire(w_pb_v[:, :, cg * 512:(cg + 1) * 512])]
            rd_T = t_ob
            for c in range(4):
                bank = rr_bank(4, 8)
                proj_fm(bank, wl, c, obT, TT)
                A("dve", lambda e, bank=bank, c=c: e.tensor_tensor(out=sbg[:, c, 0:TT], in0=ps[:, bank, 0:TT], in1=sbg[:, c, 0:TT], op=ALU.mult),
                  writes=[PB[bank], t_sb_])
                A("dve", lambda e, c=c, cg=cg: e.tensor_tensor(out=mT[:, cg * 4 + c, 0:TT], in0=sbg[:, c, 0:TT], in1=sa[:, c, 0:TT], op=ALU.add),
                  reads=[t_sa, t_sb_], writes=[t_mT])
            W.release(1)
            rd_T = T_xn[0]

        prefetch_x([0, 1])

        ysb = []
        for i in range(ntl):
            t1, a1 = AR.take("ysb%d" % i, i * 8192, 8192)
            ysb.append((t1, a1))

        def final(i, tl):
            nt = tl["nt"]
            t1, y_ap = ysb[i]
            tx, x_ap = xr[i]
            A("act", lambda e, nt=nt, y_ap=y_ap: e.activation(out=junkS[0:nt, :], in_=y_ap[0:nt, :], func=AF.Square, accum_out=stats[0:nt, 40:41]),
              reads=[t1], writes=[T_junk, T_stats])
            A("act", lambda e, nt=nt: e.activation(out=stats[0:nt, 41:42], in_=stats[0:nt, 40:41], func=AF.Sqrt, scale=1.0 / D, bias=epsc[0:nt, :]),
              reads=[T_c], writes=[T_stats])
            A("dve", lambda e, nt=nt: e.reciprocal(stats[0:nt, 42:43], stats[0:nt, 41:42]), writes=[T_stats])
            A("dve", lambda e, nt=nt, y_ap=y_ap: e.scalar_tensor_tensor(out=y_ap[0:nt, :], in0=y_ap[0:nt, :], scalar=stats[0:nt, 42:43],
                                                                        op0=ALU.mult, in1=gpost_b[0:nt, :], op1=ALU.mult),
              reads=[T_stats, T_c], writes=[t1])
            A("dve", lambda e, nt=nt, y_ap=y_ap, x_ap=x_ap: e.tensor_tensor(out=y_ap[0:nt, :], in0=y_ap[0:nt, :], in1=x_ap[0:nt, :], op=ALU.add),
              reads=[tx], writes=[t1])
            S.dma("sp", lambda e, nt=nt, y_ap=y_ap, d=ydst(tl): e.dma_start(out=d, in_=y_ap[0:nt, :]), reads=[t1])

        rd_T = t_mT
        for n in range(4):
            wl = [W.acquire(w_out_v[:, 0:8, n * 512:(n + 1) * 512]), W.acquire(w_out_v[:, 8:16, n * 512:(n + 1) * 512])]
            for i, tl in enumerate(tiles):
                nt, col = tl["nt"], tl["col"]
                bank = rr_bank(0, 8)
                proj_tm(bank, wl, mT, col, nt)
                t1, a1 = ysb[i]
                A("act", lambda e, bank=bank, nt=nt, a1=a1, n=n: e.activation(out=a1[0:nt, n * 512:(n + 1) * 512], in_=ps[0:nt, bank, :], func=AF.Copy),
                  writes=[PB[bank], t1])
                if n == 3:
                    final(i, tl)
            W.release(2)
        rd_T = T_xn[0]
        prefetch_x([2, 3])

    table_dma()
    W.start()
    for pi_, tiles_ in enumerate(passes):
        do_pass(pi_, tiles_)

    S.wait_all("sp", [(k, v) for k, v in S.dma_sems.items() if v > 0])
    return W


def build_nc(cfg):
    nc0 = bass.Bass("TRN2", target_bir_lowering=False)
    with ExitStack() as st0:
        W0 = _emit_program(nc0, Sched(), st0, None, cfg)
        plan_shapes = [(src.offset, tuple(map(tuple, src.ap)), src.tensor.name) for src in W0.rec]
    nc = bass.Bass("TRN2", target_bir_lowering=False)
    with ExitStack() as st:
        S = Sched()
        plan = _PlanProxy(plan_shapes)
        W = _emit_program(nc, S, st, plan, cfg)
        assert W.n_acq == len(plan_shapes), (W.n_acq, len(plan_shapes))
        S.emit(nc, st)
        cfg["_stats"] = dict(cnt=dict(S.cnt), dma=dict(S.dma_sems), nw=len(plan_shapes))
    return nc


class _PlanProxy:
    def __init__(self, shapes):
        self.shapes = shapes
        self.tensors = {}

    def bind(self, ap):
        self.tensors[ap.tensor.name] = ap.tensor

    def __len__(self):
        return len(self.shapes)

    def __getitem__(self, i):
        off, ap, name = self.shapes[i]
        t = self.tensors[name]
        return bass.AP(tensor=t, offset=off, ap=[list(x) for x in ap])

    def scratch(self, i):
        off, ap, name = self.shapes[i]
        t = self.tensors[name + "_bf"]
        return bass.AP(tensor=t, offset=off, ap=[list(x) for x in ap])


_NC_CACHE = {}


def _get_nc(cfg_key, cfg):
    if cfg_key not in _NC_CACHE:
        _NC_CACHE[cfg_key] = build_nc(cfg)
    return _NC_CACHE[cfg_key]


def kernel(x_prompt, x_sample, state_hgrn, cache_k, cache_v, norm_pre, w_in, lb_logits, gnorm_a, rel_bias,
           w_proj_a, w_proj_b, w_out, norm_post, _cfg=None):
    cfg = dict(n_prompt_tiles=SEQ // 128)
    if _cfg:
        cfg.update(_cfg)
    NPT = cfg["n_prompt_tiles"]
    f = lambda a: np.ascontiguousarray(np.asarray(a, dtype=np.float32))
    x_prompt, x_sample = f(x_prompt), f(x_sample)
    state_hgrn, cache_k, cache_v = f(state_hgrn), f(cache_k), f(cache_v)
    shared = dict(
        w_in=f(w_in[0]), w_pa=f(w_proj_a[0]), w_pb=f(w_proj_b[0]), w_out=f(w_out[0]),
        norm_pre=f(norm_pre[0]).reshape(1, D), norm_post=f(norm_post[0]).reshape(1, D),
        gnorm=f(gnorm_a[0]).reshape(1, DA), lb_logits=f(lb_logits), rel_bias=f(rel_bias[0]),
    )
    in_maps = []
    for c in range(NCORES):
        m = dict(shared)
        m["xp"] = np.ascontiguousarray(x_prompt[c, :NPT * 128])
        m["xs"] = np.ascontiguousarray(x_sample[2 * c:2 * c + 2])
        m["st_in"] = np.ascontiguousarray(state_hgrn[0, 2 * c:2 * c + 2])
        m["ck"] = np.ascontiguousarray(cache_k[0, 2 * c:2 * c + 2].reshape(2, 512, 1024))
        m["cv"] = np.ascontiguousarray(cache_v[0, 2 * c:2 * c + 2].reshape(2, 512, 1024))
        in_maps.append(m)
    nc = _get_nc((NPT, tuple(cfg.get('sizes') or ())), cfg)
    res = run_bass_kernel_spmd(nc, in_maps, core_ids=list(range(NCORES)))
    R = res.results
    y_prompt = np.stack([R[c]["yp"] for c in range(NCORES)], 0)
    y_sample = np.concatenate([R[c]["ys"] for c in range(NCORES)], 0)
    nsp = np.stack([R[c]["sp_o"] for c in range(NCORES)], 0)[None]
    nkp = np.stack([R[c]["kp"].reshape(512, 16, 64) for c in range(NCORES)], 0)[None]
    nvp = np.stack([R[c]["vp"].reshape(512, 16, 64) for c in range(NCORES)], 0)[None]
    nss = np.concatenate([R[c]["ss_o"] for c in range(NCORES)], 0)[None]
    nks = np.concatenate([R[c]["ks"].reshape(2, ST, 16, 64) for c in range(NCORES)], 0)[None]
    nvs = np.concatenate([R[c]["vs"].reshape(2, ST, 16, 64) for c in range(NCORES)], 0)[None]
    return (y_prompt, y_sample, nsp, nkp, nvp, nss, nks, nvs)
```
